# Optimizing a Trainium2 kernel written in Bass

```python
import jax, jax.numpy as jnp
from jax import lax
import numpy as np

D_MODEL = 1024
BATCH = 8
SEQ = 4096
DEPTH = 1

CHUNK = 64
Q_BLOCK = 128

MIX_WIDTH = D_MODEL
GLA_WIDTH = MIX_WIDTH // 2
DSA_WIDTH = MIX_WIDTH - GLA_WIDTH

GLA_HEADS = 4
GLA_DV = GLA_WIDTH // GLA_HEADS
GLA_DK = GLA_DV // 2
GLA_QK = GLA_HEADS * GLA_DK
GLA_LOWRANK = 16
GLA_TAU = 16.0

DSA_HEADS = 8
DSA_DH = DSA_WIDTH // DSA_HEADS
IDX_HEADS = 8
IDX_DIM = 64
TOPK_MAX = 256

D_FF = 2816

LN_EPS = 1e-5
RMS_EPS = 1e-6
DEEPNORM_ALPHA = (2.0 * DEPTH) ** 0.25
DEEPNORM_BETA = (8.0 * DEPTH) ** -0.25

IN_SPLITS = (GLA_QK, GLA_QK, GLA_WIDTH, GLA_WIDTH, GLA_LOWRANK,
             DSA_WIDTH, DSA_WIDTH, DSA_WIDTH, IDX_HEADS * IDX_DIM, IDX_DIM, IDX_HEADS)
IN_TOTAL = 2 * GLA_QK + 2 * GLA_WIDTH + GLA_LOWRANK + 3 * DSA_WIDTH + IDX_HEADS * IDX_DIM + IDX_DIM + IDX_HEADS

kernel_name = "hymba_gla_dsa_macaron_deepnorm"


def layer_norm(x, g, b):
    xf = x.astype(jnp.float32)
    mu = jnp.mean(xf, axis=-1, keepdims=True)
    xc = xf - mu
    var = jnp.mean(xc * xc, axis=-1, keepdims=True)
    return (xc * lax.rsqrt(var + LN_EPS) * g + b).astype(x.dtype)


def swiglu_ffn(x, w_gu, w_down):
    gu = x @ w_gu
    g, u = jnp.split(gu, 2, axis=-1)
    return (jax.nn.silu(g) * u) @ w_down


def gla_chunked(q, k, v, log_a):
    B, T, H, dk = q.shape
    dv = v.shape[-1]
    n = T // CHUNK

    def to_chunks(t):
        return t.reshape(B, n, CHUNK, H, t.shape[-1]).transpose(1, 0, 3, 2, 4)

    qc, kc, vc, ac = to_chunks(q), to_chunks(k), to_chunks(v), to_chunks(log_a)
    causal = jnp.tril(jnp.ones((CHUNK, CHUNK), dtype=bool))

    def step(S, inp):
        qn, kn, vn, an = inp
        b = jnp.cumsum(an, axis=-2)
        b_last = b[:, :, -1:, :]
        diff = b[:, :, :, None, :] - b[:, :, None, :, :]
        diff = jnp.where(causal[None, None, :, :, None], diff, -jnp.inf)
        A = jnp.einsum('bhid,bhjd,bhijd->bhij', qn, kn, jnp.exp(diff))
        o = jnp.einsum('bhij,bhjv->bhiv', A, vn) + jnp.einsum('bhid,bhdv->bhiv', qn * jnp.exp(b), S)
        S_new = jnp.exp(b_last[:, :, 0, :])[..., None] * S + jnp.einsum(
            'bhjd,bhjv->bhdv', kn * jnp.exp(b_last - b), vn)
        return S_new, o

    S0 = jnp.zeros((B, H, dk, dv), jnp.float32)
    _, o = lax.scan(step, S0, (qc, kc, vc, ac))
    return o.transpose(1, 0, 3, 2, 4).reshape(B, T, H, dv)


def dsa_attention(q, k, v, q_idx, k_idx, w_idx):
    B, T, H, Dh = q.shape
    topk = min(TOPK_MAX, T // 4)
    nb = T // Q_BLOCK
    key_chunk = jnp.arange(T) // CHUNK
    k_idx32 = k_idx.astype(jnp.float32)
    scale = Dh ** -0.5

    def to_blocks(t):
        return jnp.moveaxis(t.reshape((B, nb, Q_BLOCK) + t.shape[2:]), 1, 0)

    def gather_rows(table, idx):
        return jax.vmap(lambda tb, ib: tb[ib])(table, idx)

    def block(args):
        qb, qib, wb, start = args
        q_chunk = (start + jnp.arange(Q_BLOCK)) // CHUNK
        adm = key_chunk[None, :] <= q_chunk[:, None]
        s = jax.nn.relu(jnp.einsum('bqhd,bsd->bqhs', qib.astype(jnp.float32), k_idx32))
        score = jnp.einsum('bqh,bqhs->bqs', wb.astype(jnp.float32), s)
        score = jnp.where(adm[None], score, -jnp.inf)
        _, idx = lax.top_k(score, topk)
        valid = key_chunk[idx] <= q_chunk[None, :, None]
        kg = gather_rows(k, idx)
        vg = gather_rows(v, idx)
        logits = jnp.einsum('bqhd,bqkhd->bqhk', qb, kg).astype(jnp.float32) * scale
        logits = jnp.where(valid[:, :, None, :], logits, -jnp.inf)
        p = jax.nn.softmax(logits, axis=-1)
        return jnp.einsum('bqhk,bqkhd->bqhd', p.astype(vg.dtype), vg)

    starts = jnp.arange(nb) * Q_BLOCK
    out = lax.map(block, (to_blocks(q), to_blocks(q_idx), to_blocks(w_idx), starts))
    return jnp.moveaxis(out, 0, 1).reshape(B, T, H * Dh)


def hybrid_mixer(h, w_in, w_gla_a2, b_gla_a, gla_norm_g, w_out):
    B, T, _ = h.shape
    f32 = jnp.float32
    proj = h @ w_in
    offs = [int(o) for o in np.cumsum(IN_SPLITS)[:-1]]
    gq, gk, gv, gg, ga, dq, dk, dv, iq, ik, iw = jnp.split(proj, offs, axis=-1)

    q = gq.astype(f32).reshape(B, T, GLA_HEADS, GLA_DK) * (GLA_DK ** -0.5)
    k = gk.astype(f32).reshape(B, T, GLA_HEADS, GLA_DK)
    v = gv.astype(f32).reshape(B, T, GLA_HEADS, GLA_DV)
    log_a = (jax.nn.log_sigmoid((ga @ w_gla_a2 + b_gla_a).astype(f32)) / GLA_TAU).reshape(B, T, GLA_HEADS, GLA_DK)
    o = gla_chunked(q, k, v, log_a)
    o = o * lax.rsqrt(jnp.mean(o * o, axis=-1, keepdims=True) + RMS_EPS) * gla_norm_g.astype(f32)
    o_gla = (o.reshape(B, T, GLA_WIDTH) * jax.nn.silu(gg.astype(f32))).astype(h.dtype)

    o_dsa = dsa_attention(
        dq.reshape(B, T, DSA_HEADS, DSA_DH),
        dk.reshape(B, T, DSA_HEADS, DSA_DH),
        dv.reshape(B, T, DSA_HEADS, DSA_DH),
        iq.reshape(B, T, IDX_HEADS, IDX_DIM) * (IDX_DIM ** -0.5),
        ik,
        iw * (IDX_HEADS ** -0.5),
    ).astype(h.dtype)

    return jnp.concatenate([o_gla, o_dsa], axis=-1) @ w_out


def setup_inputs(seed: int = 0) -> dict:
    key = jax.random.key(seed)
    ks = jax.random.split(key, 17)
    nrm = lambda k, shape, fan_in, gain=1.0: jax.random.normal(k, shape, jnp.float32) * (gain * fan_in ** -0.5)
    return {
        "x": jax.random.normal(ks[0], (BATCH, SEQ, D_MODEL), jnp.float32),
        "w_in": nrm(ks[1], (DEPTH, D_MODEL, IN_TOTAL), D_MODEL),
        "w_gla_a2": nrm(ks[2], (DEPTH, GLA_LOWRANK, GLA_QK), GLA_LOWRANK),
        "b_gla_a": 0.1 * jax.random.normal(ks[3], (DEPTH, GLA_QK), jnp.float32),
        "gla_norm_g": 1.0 + 0.02 * jax.random.normal(ks[4], (DEPTH, GLA_DV), jnp.float32),
        "w_out": nrm(ks[5], (DEPTH, MIX_WIDTH, D_MODEL), MIX_WIDTH, DEEPNORM_BETA),
        "ffn1_w_gu": nrm(ks[6], (DEPTH, D_MODEL, 2 * D_FF), D_MODEL),
        "ffn1_w_down": nrm(ks[7], (DEPTH, D_FF, D_MODEL), D_FF, DEEPNORM_BETA),
        "ffn2_w_gu": nrm(ks[8], (DEPTH, D_MODEL, 2 * D_FF), D_MODEL),
        "ffn2_w_down": nrm(ks[9], (DEPTH, D_FF, D_MODEL), D_FF, DEEPNORM_BETA),
        "ln1_g": 1.0 + 0.02 * jax.random.normal(ks[10], (DEPTH, D_MODEL), jnp.float32),
        "ln1_b": 0.02 * jax.random.normal(ks[11], (DEPTH, D_MODEL), jnp.float32),
        "ln2_g": 1.0 + 0.02 * jax.random.normal(ks[12], (DEPTH, D_MODEL), jnp.float32),
        "ln2_b": 0.02 * jax.random.normal(ks[13], (DEPTH, D_MODEL), jnp.float32),
        "ln3_g": 1.0 + 0.02 * jax.random.normal(ks[14], (DEPTH, D_MODEL), jnp.float32),
        "ln3_b": 0.02 * jax.random.normal(ks[15], (DEPTH, D_MODEL), jnp.float32),
    }


def reference(x, w_in, w_gla_a2, b_gla_a, gla_norm_g, w_out, ffn1_w_gu, ffn1_w_down,
              ffn2_w_gu, ffn2_w_down, ln1_g, ln1_b, ln2_g, ln2_b, ln3_g, ln3_b):
    for l in range(DEPTH):
        x = layer_norm(DEEPNORM_ALPHA * x + 0.5 * swiglu_ffn(x, ffn1_w_gu[l], ffn1_w_down[l]), ln1_g[l], ln1_b[l])
        x = layer_norm(DEEPNORM_ALPHA * x + hybrid_mixer(x, w_in[l], w_gla_a2[l], b_gla_a[l], gla_norm_g[l], w_out[l]),
                       ln2_g[l], ln2_b[l])
        x = layer_norm(DEEPNORM_ALPHA * x + 0.5 * swiglu_ffn(x, ffn2_w_gu[l], ffn2_w_down[l]), ln3_g[l], ln3_b[l])
    return x
```

```python
from contextlib import ExitStack

import numpy as np
import concourse.bass as bass
import concourse.mybir as mybir
from concourse.alu_op_type import AluOpType as ALU
from concourse.bass_utils import run_bass_kernel_spmd

F32 = mybir.dt.float32
BF16 = mybir.dt.bfloat16
AF = mybir.ActivationFunctionType
AX = mybir.AxisListType

D = 1024
DFF = 2816
NFC = 22
SEQ = 4096
NCORES = 8
STK = 512
ALPHA = 2.0 ** 0.25
C_FFN = 0.5 / ALPHA
C_MIX = 1.0 / ALPHA
EPS_LN = 1e-5 / (ALPHA * ALPHA)
RMS_EPS = 1e-6
C_IDX = (64.0 ** -0.5) * (8.0 ** -0.5)
NEG = -30000.0
KIT = 12
TOPK = 256
RING = 3
NCH = 22 + 12 + 9 + 8 + 4 + 22 + 12

CF_ID = 0
CF_TRIN = 128
CF_TRIP = 256
CF_CM = 384
CF_WA2 = 512
CF_BA = 768
CF_GN = 1024
CF_P2 = 1152
NCF = 1152 + 32


class Buf:
    __slots__ = ("name", "w", "r", "sem", "semval", "excl")

    def __init__(self, name, excl=False):
        self.name = name
        self.excl = excl
        self.w = None
        self.r = []
        self.sem = None
        self.semval = 0


class Op:
    __slots__ = ("eng", "idx", "fn", "deps", "dma", "signal", "sigcount", "semval")

    def __init__(self, eng, idx, fn, deps, dma):
        self.eng = eng
        self.idx = idx
        self.fn = fn
        self.deps = deps
        self.dma = dma
        self.signal = False
        self.sigcount = 0
        self.semval = 0


ENGS = ("pe", "act", "dve", "pool", "sp")


class Sched:
    def __init__(self):
        self.ops = {e: [] for e in ENGS}

    def add(self, eng, fn, reads=(), writes=(), dma=None):
        deps = {}
        for b in reads:
            if b.w is not None:
                deps[id(b.w)] = b.w
            if b.excl:
                for r in b.r:
                    if r.eng != eng:
                        deps[id(r)] = r
        for b in writes:
            if b.w is not None:
                deps[id(b.w)] = b.w
            for r in b.r:
                deps[id(r)] = r
        op = Op(eng, len(self.ops[eng]), fn, list(deps.values()), dma)
        if dma is not None:
            dma.semval += 16
            op.semval = dma.semval
        for b in reads:
            b.r.append(op)
        for b in writes:
            b.w = op
            b.r = []
        self.ops[eng].append(op)
        return op

    def emit(self, nc, es):
        def plan(eng_name, mark):
            waited = {}
            out = []
            for op in self.ops[eng_name]:
                need = {}
                for d in op.deps:
                    if d.dma is not None:
                        key = ("d", id(d.dma))
                        val = d.semval
                    else:
                        if d.eng == "pe" and eng_name == "pe" and op.dma is None:
                            continue
                        key = ("e", d.eng)
                        val = d.idx
                    if waited.get(key, -1) >= val:
                        continue
                    if key not in need or need[key][0] < val:
                        need[key] = (val, d)
                for key, (val, d) in need.items():
                    waited[key] = val
                    if mark and d.dma is None:
                        d.signal = True
                out.append(list(need.values()))
            return out

        for e in ENGS:
            plan(e, True)
        for e in ENGS:
            cnt = 0
            for op in self.ops[e]:
                if op.dma is None and op.signal:
                    cnt += 1
                    op.sigcount = cnt
        plans = {e: plan(e, False) for e in ENGS}
        sems = {e: es.enter_context(nc.semaphore("sem_" + e)) for e in ENGS}
        dma_bufs = {}
        for e in ENGS:
            for op in self.ops[e]:
                if op.dma is not None and id(op.dma) not in dma_bufs:
                    dma_bufs[id(op.dma)] = op.dma
        for b in dma_bufs.values():
            b.sem = es.enter_context(nc.semaphore("dsem_" + b.name))
        block = es.enter_context(nc.Block())

        def run_engine(eng_name, handle):
            for op, needs in zip(self.ops[eng_name], plans[eng_name]):
                for val, d in needs:
                    if d.dma is not None:
                        handle.wait_ge(d.dma.sem, d.semval)
                    else:
                        handle.wait_ge(sems[d.eng], d.sigcount)
                ins = op.fn(handle)
                if op.dma is not None:
                    ins.then_inc(op.dma.sem, 16)
                elif op.signal:
                    ins.then_inc(sems[eng_name], 1)

        @block.tensor
        def _(h):
            run_engine("pe", h)

        @block.scalar
        def _(h):
            run_engine("act", h)

        @block.vector
        def _(h):
            run_engine("dve", h)

        @block.gpsimd
        def _(h):
            run_engine("pool", h)

        @block.sync
        def _(h):
            run_engine("sp", h)


def build(T, dbg=False, stop=99):
    NST = T // STK
    NQB = T // 128
    nc = bass.Bass("TRN2", target_bir_lowering=False)
    x_d = nc.dram_tensor("x", [T, D], F32, kind="ExternalInput").ap()
    ws_d = nc.dram_tensor("ws", [NCH, 128, 2048], F32, kind="ExternalInput").ap()
    cf_d = nc.dram_tensor("cf", [128, NCF], F32, kind="ExternalInput").ap()
    lnp_d = nc.dram_tensor("lnp", [6, 128, D], F32, kind="ExternalInput").ap()
    out_d = nc.dram_tensor("out", [T, D], F32, kind="ExternalOutput").ap()
    if dbg:
        dbg1_d = nc.dram_tensor("dbg1", [T, D], F32, kind="ExternalOutput").ap()
        dbg2_d = nc.dram_tensor("dbg2", [T, D], F32, kind="ExternalOutput").ap()
        dbg3_d = nc.dram_tensor("dbg3", [T, D], F32, kind="ExternalOutput").ap()

    es = ExitStack()
    S = Sched()

    def sb(name, shape, dt):
        return es.enter_context(nc.sbuf_tensor("sb_" + name, shape, dt))

    Kc = sb("Kc", [128, 4, T], BF16)
    Vc = sb("Vc", [128, NQB, 520], BF16)
    Kix = sb("Kix", [128, T], BF16)
    ring = sb("ring", [128, RING, 2048], BF16)
    res = sb("res", [128, 4, D], F32)
    xT = sb("xT", [128, 8, STK], BF16)
    bigb = sb("bigb", [128, 16384], BF16)
    acc = sb("acc", [128, 4096], F32)
    rh = sb("rh", [128, 2, 512], F32)
    PT = sb("PT", [128, 4, 512], BF16)
    gb = sb("gb", [128, 2, D], F32)
    omix = sb("omix", [128, D], F32)
    omixg = sb("omixg", [128, 2, 512], F32)
    cf = sb("cf", [128, NCF], F32)
    identb = sb("identb", [128, 128], BF16)
    ident4 = sb("ident4", [128, 512], BF16)
    qgbd = sb("qgbd", [128, 2, 4, 256], BF16)
    kgT = sb("kgT", [128, 2, STK], BF16)
    ktok = sb("ktok", [128, 4, 256], BF16)
    vg = sb("vg", [128, 4, 512], BF16)
    gsil = sb("gsil", [128, 4, 512], BF16)
    ATt = sb("ATt", [128, 512], BF16)
    Sst = sb("Sst", [128, 2, 128], F32)
    Sbt = sb("Sbt", [128, 2, 128], BF16)
    wabs = sb("wabs", [128, 4, 8], F32)
    wsgn = sb("wsgn", [128, 4, 8], F32)
    small = sb("small", [128, 256], F32)
    tauc = sb("tauc", [128, 1], F32)
    seg8 = sb("seg8", [128, 32, 8], F32)

    aT = bigb[:, 0:NFC * 512].rearrange("p (f t) -> p f t", f=NFC)
    mbs = [bigb[:, 0:4096], bigb[:, 12288:16384]]
    qz = bigb[:, 4096:8192].rearrange("p (h t) -> p h t", h=8)
    qbd = bigb[:, 8192:12288].rearrange("p (a b c) -> p a b c", a=4, b=4)
    E1 = acc[:, 0:1024].rearrange("p (a t) -> p a t", a=2)
    E2 = acc[:, 1024:2048].rearrange("p (a t) -> p a t", a=2)
    E2tok = acc[:, 2048:3072].rearrange("p (a t) -> p a t", a=4)
    gaT = acc[:, 3072:3584]
    spt = acc[:, 3584:3840]
    gbv = gb[:, 0, :].bitcast(BF16)
    Dg = gbv[:, 0:1024].rearrange("p (h q) -> p h q", h=8)
    rhb = gb[:, 1, :].bitcast(BF16).rearrange("p (a t) -> p a t", a=4)
    identf = cf[:, CF_ID:CF_ID + 128]
    trin = cf[:, CF_TRIN:CF_TRIN + 128]
    trip = cf[:, CF_TRIP:CF_TRIP + 128]
    cmask = cf[:, CF_CM:CF_CM + 128]
    wa2p = cf[:, CF_WA2:CF_WA2 + 256]
    ba_b = cf[:, CF_BA:CF_BA + 256]
    gn = cf[:, CF_GN:CF_GN + 128]
    pow2 = cf[:, CF_P2:CF_P2 + KIT + 1]

    def sm(a, b):
        return small[:, a:b]

    ps = [es.enter_context(nc.psum_tensor("ps%d" % i, [128, 512], F32)) for i in range(8)]
    PB = [Buf("ps%d" % i, excl=True) for i in range(8)]

    B = {}

    def bf(name):
        if name not in B:
            B[name] = Buf(name)
        return B[name]

    cfB = bf("cf")
    resB = [bf("res%d" % i) for i in range(4)]
    xTB = [bf("xT%d" % i) for i in range(8)]
    ringB = [bf("ring%d" % i) for i in range(RING)]
    bigB = bf("bigb")
    aTB = [bf("aT%d" % i) for i in range(NFC)]
    accB = bf("acc")
    rhB = [bf("rh0"), bf("rh1")]
    PTB = [bf("PT%d" % i) for i in range(4)]
    gbB = bf("gb")
    gbP = bf("gbP")
    rhbB = [bf("rhb%d" % i) for i in range(4)]
    omixB = bf("omix")
    omixgB = [bf("omixg0"), bf("omixg1")]
    KcB = [bf("Kc%d" % i) for i in range(NST)]
    VcB = [bf("Vc%d" % i) for i in range(NST)]
    KixB = [bf("Kix%d" % i) for i in range(NST)]

    bank_rr = [0]

    def next_bank(pool=(0, 1, 2, 3, 4, 5, 6, 7)):
        bank_rr[0] += 1
        return pool[bank_rr[0] % len(pool)]

    def mm(out, lhsT, rhs, start, stop, reads, writes, skip=False):
        S.add("pe", lambda e: e.matmul(out, lhsT, rhs, start=start, stop=stop, skip_group_check=skip), reads, writes)

    def tr(out, in_, reads, writes):
        S.add("pe", lambda e: e.transpose(out, in_, identf), list(reads) + [cfB], writes)

    def act(out, in_, func, reads, writes, scale=1.0, bias=0.0):
        S.add("act", lambda e: e.activation(out, in_, func, bias=bias, scale=scale), reads, writes)

    def cp(eng, out, in_, reads, writes):
        if eng == "act":
            S.add("act", lambda e: e.activation(out, in_, AF.Copy), reads, writes)
        else:
            S.add(eng, lambda e: e.tensor_copy(out, in_), reads, writes)

    def tt(eng, out, in0, in1, op, reads, writes):
        S.add(eng, lambda e: e.tensor_tensor(out, in0, in1, op), reads, writes)

    def ts(eng, out, in0, s1, s2, op0, op1, reads, writes, accum=None):
        if op1 is None:
            S.add(eng, lambda e: e.tensor_scalar(out, in0, s1, None, op0), reads, writes)
        else:
            S.add(eng, lambda e: e.tensor_scalar(out, in0, s1, s2, op0, op1, accum_out=accum), reads, writes)

    def stt(out, in0, scalar, in1, op0, op1, reads, writes):
        S.add("dve", lambda e: e.scalar_tensor_tensor(out, in0, scalar, in1, op0, op1), reads, writes)

    evac_rr = [0]

    def evac_eng():
        evac_rr[0] += 1
        return "act" if evac_rr[0] % 2 else "dve"

    wstate = {"issued": 0, "cur": 0}
    total_chunks = NST * NCH

    def w_issue():
        j = wstate["issued"]
        if j >= total_chunks:
            return
        slot = j % RING
        src = ws_d[j % NCH]
        S.add("pool", lambda e: e.dma_start(out=ring[:, slot, :], in_=src, max_dma_last_dim=4096),
              reads=[], writes=[ringB[slot]], dma=ringB[slot])
        wstate["issued"] = j + 1

    def w_next():
        j = wstate["cur"]
        while wstate["issued"] < min(j + RING - 1, total_chunks) or wstate["issued"] <= j:
            w_issue()
        wstate["cur"] = j + 1
        slot = j % RING
        return ring[:, slot, :], ringB[slot]

    S.add("sp", lambda e: e.dma_start(out=cf[:], in_=cf_d), reads=[], writes=[cfB], dma=cfB)
    identB_ = bf("identb")
    cp("dve", identb[:], identf, [cfB], [identB_])
    for i in range(4):
        cp("dve", ident4[:, i * 128:(i + 1) * 128], identf, [cfB], [identB_])
    S.add("dve", lambda e: e.memset(bigb[:, 4096:12288], 0.0), [], [bigB])
    S.add("dve", lambda e: e.memset(Vc[:], 1.0), [], VcB)
    S.add("dve", lambda e: e.memset(Sst[:], 0.0), [], [bf("S")])
    S.add("dve", lambda e: e.memset(Sbt[:], 0.0), [], [bf("Sb")])
    S.add("dve", lambda e: e.memset(tauc[:], -1e29), [], [bf("tauc")])
    S.add("dve", lambda e: e.memset(qgbd[:], 0.0), [], [bf("qgbd")])

    def tile_to_xT(src, src_bufs, t4):
        tsl = slice(t4 * 128, (t4 + 1) * 128)
        for half in range(2):
            bank = next_bank((0, 1, 2, 3))
            for i in range(4):
                c = half * 4 + i
                tr(ps[bank][:, i * 128:(i + 1) * 128], src[:, c * 128:(c + 1) * 128], src_bufs, [PB[bank]])
            cp(evac_eng(), xT[:, half * 4:half * 4 + 4, tsl], ps[bank][:].rearrange("p (a t) -> p a t", a=4),
               [PB[bank]], [xTB[half * 4 + i] for i in range(4)])

    def to_xT(_unused):
        for t4 in range(4):
            tile_to_xT(res[:, t4, :], [resB[t4]], t4)

    stg = [omix[:, :], rh[:, :, :].rearrange("p a t -> p (a t)")]
    stgB = [[omixB], [rhB[0], rhB[1]]]

    def prefetch_x_tile(s_next, t4):
        k = t4 % 2
        src = x_d[s_next * STK + t4 * 128:s_next * STK + (t4 + 1) * 128, :]
        S.add("sp", lambda e: e.dma_start(out=stg[k], in_=src), reads=[], writes=stgB[k], dma=bf("stgd%d" % k))
        tile_to_xT(stg[k], stgB[k], t4)

    ln_hoisted = set()

    def ln_stage(i, t4):
        st6 = sm(t4 * 16, t4 * 16 + 12)
        mv = sm(t4 * 16 + 12, t4 * 16 + 14)
        lnv = sm(t4 * 16 + 14, t4 * 16 + 15)
        rstd = sm(t4 * 16 + 15, t4 * 16 + 16)
        nmr = sm(160 + t4, 161 + t4)
        sB = bf("lnstat%d" % t4)
        r = res[:, t4, :]
        if i == -1:
            S.add("dve", lambda e: e.bn_stats(st6[:, 0:6], res[:, t4, 0:512]), [resB[t4]], [sB])
            ln_hoisted.add(t4)
        elif i == 0:
            if t4 in ln_hoisted:
                ln_hoisted.discard(t4)
            else:
                S.add("dve", lambda e: e.bn_stats(st6[:, 0:6], res[:, t4, 0:512]), [resB[t4]], [sB])
            S.add("dve", lambda e: e.bn_stats(st6[:, 6:12], res[:, t4, 512:1024]), [resB[t4]], [sB])
            S.add("dve", lambda e: e.bn_aggr(mv, st6), [sB], [sB])
            ts("dve", lnv, mv[:, 1:2], EPS_LN, None, ALU.add, None, [sB], [sB])
        elif i == 1:
            act(lnv, lnv, AF.Ln, [sB], [sB])
            act(rstd, lnv, AF.Exp, [sB], [sB], scale=-0.5)
        elif i == 2:
            stt(nmr, mv[:, 0:1], -1.0, rstd, ALU.mult, ALU.mult, [sB], [sB])
        elif i == 3:
            act(r, r, AF.Identity, [sB, resB[t4]], [resB[t4]], scale=rstd, bias=nmr)
        else:
            tt("dve", r, r, gb[:, 0, :], ALU.mult, [resB[t4], gbB, gbP], [resB[t4]])
            tt("dve", r, r, gb[:, 1, :], ALU.add, [resB[t4], gbB, gbP], [resB[t4]])

    def layer_norm_all(k, first=0):
        for step in range(first, 5 + 3):
            for t4 in range(4):
                i = step - t4
                if first <= i <= 4:
                    ln_stage(i, t4)

    def load_gb(k):
        for j in range(2):
            src = lnp_d[2 * k + j]
            S.add("sp", (lambda e, j=j, src=src: e.dma_start(out=gb[:, j, :], in_=src)), reads=[], writes=[gbB, gbP], dma=gbB)

    def ffn(k, down_hook=None, gu_hook=None, do_ln=True):
        for fc in range(NFC):
            w, wB = w_next()
            wv = w.rearrange("p (c f) -> p c f", c=8)
            bg = (0, 2)[fc % 2]
            bu = (1, 3)[fc % 2]
            for c in range(8):
                mm(ps[bg][:], wv[:, c, 0:128], xT[:, c, :], c == 0, c == 7, [wB, xTB[c]], [PB[bg]])
            for c in range(8):
                mm(ps[bu][:], wv[:, c, 128:256], xT[:, c, :], c == 0, c == 7, [wB, xTB[c]], [PB[bu]])
            sl = fc % 2
            act(rh[:, sl, :], ps[bg][:], AF.Silu, [PB[bg]], [rhB[sl]])
            tt("dve", aT[:, fc, :], rh[:, sl, :], ps[bu][:], ALU.mult, [rhB[sl], PB[bu], bigB], [aTB[fc]])
            if gu_hook is not None:
                gu_hook(fc)
        for dh in range(2):
            for g in range(6):
                w, wB = w_next()
                wv = w.rearrange("p (i f) -> p i f", i=4)
                for i in range(4):
                    fc = 4 * g + i
                    if fc >= NFC:
                        break
                    for t4 in range(4):
                        mm(ps[4 + t4][:], aT[:, fc, t4 * 128:(t4 + 1) * 128], wv[:, i, :], fc == 0, fc == NFC - 1,
                           [wB, aTB[fc]], [PB[4 + t4]])
                if down_hook is not None:
                    down_hook(dh, g)
            for t4 in range(4):
                r = res[:, t4, dh * 512:(dh + 1) * 512]
                stt(r, ps[4 + t4][:], C_FFN, r, ALU.mult, ALU.add, [PB[4 + t4], resB[t4]], [resB[t4]])
                if dh == 1 and do_ln:
                    ln_stage(0, t4)
            if dh == 0:
                for t4 in range(4):
                    ln_stage(-1, t4)
        if do_ln:
            layer_norm_all(k, first=1)

    def barrier_bigb(reads_extra=()):
        d = sm(200, 201)
        S.add("dve", lambda e: e.memset(d, 0.0), [], list(aTB) + [bigB, gbP, bf("dummy")])

    def zero_qpads():
        S.add("dve", lambda e: e.memset(bigb[:, 4096:12288], 0.0), [bigB], [bf("qz"), bf("qbd")])

    def projection(s, after_iw=None):
        tok0 = s * STK
        qgB, kgB, qbdB, qzB = bf("qgbd"), bf("kgT"), bf("qbd"), bf("qz")
        def fm_chunk(names):
            w, wB = w_next()
            wv = w.rearrange("p (c f) -> p c f", c=8)
            for half in range(2):
                name = names[half]
                bank = next_bank()
                for c in range(8):
                    mm(ps[bank][:], wv[:, c, half * 128:(half + 1) * 128], xT[:, c, :], c == 0, c == 7,
                       [wB, xTB[c]], [PB[bank]])
                pb = PB[bank]
                pv = ps[bank]
                if name == "ga":
                    cp("act", gaT, pv[:], [pb], [accB])
                    for t4 in range(4):
                        b2 = next_bank()
                        mm(ps[b2][:, 0:256], gaT[:, t4 * 128:(t4 + 1) * 128], wa2p, True, True, [accB, cfB], [PB[b2]])
                        tt("dve", spt, ps[b2][:, 0:256], ba_b, ALU.add, [PB[b2], cfB], [accB])
                        act(spt, spt, AF.Exp, [accB], [accB], scale=-1.0)
                        ts("dve", spt, spt, 1.0, None, ALU.add, None, [accB], [accB])
                        act(spt, spt, AF.Ln, [accB], [accB])
                        b3 = next_bank()
                        for p in range(2):
                            mm(ps[b3][:, p * 128:(p + 1) * 128], spt[:, p * 128:(p + 1) * 128], trin, p == 0, p == 1,
                               [accB, cfB], [PB[b3]])
                        for p in range(2):
                            act(sm(120 + p * 4 + t4, 121 + p * 4 + t4), ps[b3][:, p * 128 + 127:p * 128 + 128], AF.Exp,
                                [PB[b3]], [bf("ebl")])
                            act(E1[:, p, t4 * 128:(t4 + 1) * 128], ps[b3][:, p * 128:(p + 1) * 128], AF.Exp,
                                [PB[b3]], [accB])
                            act(E2[:, p, t4 * 128:(t4 + 1) * 128], ps[b3][:, p * 128:(p + 1) * 128], AF.Exp,
                                [PB[b3]], [accB], scale=-1.0)
                        b4 = next_bank()
                        mm(ps[b4][:, 0:256], trip, spt, True, True, [accB, cfB], [PB[b4]])
                        act(E2tok[:, t4, :], ps[b4][:, 0:256], AF.Exp, [PB[b4]], [accB])
                elif name == "ik":
                    cp(evac_eng(), Kix[:, tok0:tok0 + STK], pv[:], [pb], [KixB[s]])
                elif name.startswith("gq"):
                    p = int(name[2])
                    for hh in range(2):
                        rows = slice(hh * 64, (hh + 1) * 64)
                        stt(qgbd[rows, p, :, hh * 128:(hh + 1) * 128],
                            pv[rows, :].rearrange("p (a t) -> p a t", a=4), 0.125,
                            E1[rows, p, :].rearrange("p (a t) -> p a t", a=4), ALU.mult, ALU.mult,
                            [pb, accB], [qgB])
                elif name.startswith("gk"):
                    p = int(name[2])
                    tt("dve", kgT[:, p, :], pv[:], E2[:, p, :], ALU.mult, [pb, accB], [kgB])
                elif name.startswith("dq"):
                    p = int(name[2])
                    for hh in range(2):
                        rows = slice(hh * 64, (hh + 1) * 64)
                        cp("act", qbd[rows, p, :, hh * 128:(hh + 1) * 128],
                           pv[rows, :].rearrange("p (a t) -> p a t", a=4), [pb, bigB], [qbdB])
                elif name.startswith("dk"):
                    p = int(name[2])
                    cp("act", Kc[:, p, tok0:tok0 + STK], pv[:], [pb], [KcB[s]])
                elif name.startswith("iq"):
                    p = int(name[2])
                    for hh in range(2):
                        rows = slice(hh * 64, (hh + 1) * 64)
                        cp(evac_eng(), qz[rows, 2 * p + hh, :], pv[rows, :], [pb, bigB], [qzB])
        ktB, vgB, gsB, wB_ = bf("ktok"), bf("vg"), bf("gsil"), bf("wabs")

        def tm_chunk(name):
            w, wB = w_next()
            wv = w.rearrange("p (c f) -> p c f", c=8)
            ncol = 8 if name == "iw" else 256
            for t4 in range(4):
                bank = next_bank()
                for c in range(8):
                    mm(ps[bank][:, 0:ncol], xT[:, c, t4 * 128:(t4 + 1) * 128], wv[:, c, 0:ncol], c == 0, c == 7,
                       [wB, xTB[c]], [PB[bank]])
                pb = PB[bank]
                pv = ps[bank][:, 0:ncol]
                if name == "gk":
                    tt("dve", ktok[:, t4, :], pv, E2tok[:, t4, :], ALU.mult, [pb, accB], [ktB])
                elif name.startswith("gv"):
                    i = int(name[2])
                    cp("act", vg[:, t4, i * 256:(i + 1) * 256], pv, [pb], [vgB])
                elif name.startswith("gg"):
                    i = int(name[2])
                    sl = t4 % 2
                    act(rh[:, sl, 0:256], pv, AF.Silu, [pb], [rhB[sl]])
                    tt("dve", gsil[:, t4, i * 256:(i + 1) * 256].rearrange("p (h v) -> p h v", h=2),
                       rh[:, sl, 0:256].rearrange("p (h v) -> p h v", h=2),
                       gn.unsqueeze(1).to_broadcast([128, 2, 128]), ALU.mult, [rhB[sl], cfB], [gsB])
                elif name.startswith("dv"):
                    i = int(name[2])
                    dst = Vc[:, s * 4 + t4, :].rearrange("p (h e) -> p h e", e=65)[:, 4 * i:4 * i + 4, 0:64]
                    cp("act", dst, pv.rearrange("p (h e) -> p h e", e=64), [pb], [VcB[s]])
                elif name == "iw":
                    ts("dve", wsgn[:, t4, :], pv, 0.0, 2.0, ALU.is_ge, ALU.mult, [pb], [wB_])
                    ts("dve", wsgn[:, t4, :], wsgn[:, t4, :], -1.0, None, ALU.add, None, [wB_], [wB_])
                    stt(wabs[:, t4, :], pv, C_IDX, wsgn[:, t4, :], ALU.mult, ALU.mult, [pb, wB_], [wB_])

        for names in (("ga", "ik"), ("gq0", "gq1"), ("gk0", "gk1"), ("iq0", "iq1"), ("iq2", "iq3")):
            fm_chunk(names)
        tm_chunk("gk")
        tm_chunk("gg0")
        tm_chunk("gg1")
        tm_chunk("iw")
        if after_iw is not None:
            after_iw()
        for names in (("dq0", "dq1"), ("dq2", "dq3"), ("dk0", "dk1"), ("dk2", "dk3")):
            fm_chunk(names)
        for name in ("gv0", "gv1", "dv0", "dv1"):
            tm_chunk(name)

    def gla(s, t4):
        qgB, kgB, ktB, vgB, gsB = bf("qgbd"), bf("kgT"), bf("ktok"), bf("vg"), bf("gsil")
        SB_, SbB, ATB = bf("S"), bf("Sb"), bf("AT")
        tsl = slice(t4 * 128, (t4 + 1) * 128)
        bA = next_bank()
        for p in range(2):
            mm(ps[bA][:, p * 256:(p + 1) * 256], kgT[:, p, tsl], qgbd[:, p, t4, :], p == 0, p == 1,
               [kgB, qgB], [PB[bA]])
        tt("dve", ATt[:].rearrange("p (h t) -> p h t", h=4), ps[bA][:].rearrange("p (h t) -> p h t", h=4),
           cmask.unsqueeze(1).to_broadcast([128, 4, 128]), ALU.mult, [PB[bA], cfB], [ATB])
        bo = next_bank()
        for h in range(4):
            p = h // 2
            mm(ps[bo][:, h * 128:(h + 1) * 128], ATt[:, h * 128:(h + 1) * 128], vg[:, t4, h * 128:(h + 1) * 128],
               h == 0, False, [ATB, vgB], [PB[bo]])
            mm(ps[bo][:, h * 128:(h + 1) * 128], qgbd[:, p, t4, (h % 2) * 128:(h % 2 + 1) * 128], Sbt[:, p, :],
               False, h == 3, [qgB, SbB], [PB[bo]])
        bS = next_bank()
        for p in range(2):
            mm(ps[bS][:, p * 256:(p + 1) * 256], ktok[:, t4, p * 128:(p + 1) * 128], vg[:, t4, p * 256:(p + 1) * 256],
               p == 0, p == 1, [ktB, vgB], [PB[bS]])
        for p in range(2):
            for hh in range(2):
                rows = slice(hh * 64, (hh + 1) * 64)
                tt("dve", Sst[rows, p, :], Sst[rows, p, :], ps[bS][rows, p * 256 + hh * 128:p * 256 + (hh + 1) * 128],
                   ALU.add, [SB_, PB[bS]], [SB_])
                ts("dve", Sst[rows, p, :], Sst[rows, p, :], small[rows, 120 + p * 4 + t4:121 + p * 4 + t4], None,
                   ALU.mult, None, [SB_, bf("ebl")], [SB_])
            cp("act", Sbt[:, p, :], Sst[:, p, :], [SB_], [SbB])
        ot = rh[:, 0, :]
        sq = rh[:, 1, :]
        ssq = sm(64, 68)
        rs = sm(68, 72)
        gB = bf("glastat")
        cp("act", ot, ps[bo][:], [PB[bo]], [rhB[0]])
        tt("dve", sq, ot, ot, ALU.mult, [rhB[0]], [rhB[1]])
        S.add("dve", lambda e: e.tensor_reduce(ssq, sq.rearrange("p (h v) -> p h v", h=4), AX.X, ALU.add),
              [rhB[1]], [gB])
        ts("dve", ssq, ssq, 1.0 / 128.0, RMS_EPS, ALU.mult, ALU.add, [gB], [gB])
        act(ssq, ssq, AF.Ln, [gB], [gB])
        act(rs, ssq, AF.Exp, [gB], [gB], scale=-0.5)
        tt("dve", ot.rearrange("p (h v) -> p h v", h=4), ot.rearrange("p (h v) -> p h v", h=4),
           rs.unsqueeze(2).to_broadcast([128, 4, 128]), ALU.mult, [rhB[0], gB], [rhB[0]])
        tt("dve", omixg[:, t4 % 2, :], ot, gsil[:, t4, :], ALU.mult, [rhB[0], gsB], [omixgB[t4 % 2]])

    def dsa_select(s, t4):
        qb = s * 4 + t4
        N = (qb + 1) * 128
        nkt = (N + 511) // 512
        tsl = slice(t4 * 128, (t4 + 1) * 128)
        mb = mbs[qb % 2]
        qzB, wB_, mbB, tauB = bf("qz"), bf("wabs"), bf("mb%d" % (qb % 2)), bf("tau")
        kixr = [KixB[i] for i in range(s + 1)]
        DB = bf("Dg")
        for h in range(8):
            ts("dve", Dg[:, h, :], identb[:], wsgn[:, t4, h:h + 1], None, ALU.mult, None,
               [bf("identb"), wB_, gbP], [DB])
        LAG = 3
        steps = [(kt, h) for kt in range(nkt) for h in range(8)]
        info = {}

        def score(i):
            kt, h = steps[i]
            ncol = min(512, N - kt * 512)
            bank = next_bank((2, 3, 4, 5, 6, 7))
            sl = i % 4
            mm(ps[bank][:, 0:ncol], qz[:, h, tsl], Kix[:, kt * 512:kt * 512 + ncol], True, True, [qzB] + kixr,
               [PB[bank]])
            if i % 2 == 0:
                act(rhb[:, sl, 0:ncol], ps[bank][:, 0:ncol], AF.Relu, [PB[bank], wB_, gbP], [rhbB[sl]],
                    scale=wabs[:, t4, h:h + 1])
            else:
                ts("dve", rhb[:, sl, 0:ncol], ps[bank][:, 0:ncol], wabs[:, t4, h:h + 1], 0.0, ALU.mult, ALU.max,
                   [PB[bank], wB_, gbP], [rhbB[sl]])

        def hsum(i):
            kt, h = steps[i]
            ncol = min(512, N - kt * 512)
            bsum = (0, 1)[kt % 2]
            sl = i % 4
            mm(ps[bsum][:, 0:ncol], Dg[:, h, :], rhb[:, sl, 0:ncol], h == 0, h == 7, [DB, rhbB[sl], gbP], [PB[bsum]])
            if h == 7:
                cp("act", acc[:, kt * 512:kt * 512 + ncol], ps[bsum][:, 0:ncol], [PB[bsum]], [accB])

        for i in range(len(steps) + LAG):
            if i < len(steps):
                score(i)
            if i >= LAG:
                hsum(i - LAG)
        a = acc[:, 0:N]
        if qb >= 2:
            mn = sm(80, 81)
            mx8 = sm(88, 96)
            cc = sm(81, 82)
            h0 = sm(82, 83)
            cnt = sm(83, 84)
            pm = sm(84, 85)
            hk = sm(96, 96 + KIT + 1)
            tau = sm(85, 86)
            S.add("dve", lambda e: e.memset(acc[0:64, N - 64:N], -1e30), [], [accB])
            segv = a.rearrange("p (j g) -> p g j", g=32)
            for g in range(32):
                S.add("dve", lambda e, g=g: e.max(seg8[:, g, :], segv[:, g, :]), [accB], [bf("seg%d" % g)])
            segB = [bf("seg%d" % g) for g in range(32)]
            S.add("dve", lambda e: e.tensor_reduce(mn, seg8[:, :, 7], AX.X, ALU.min), segB, [tauB])
            S.add("dve", lambda e: e.tensor_reduce(mx8[:, 0:1], seg8[:, :, 7], AX.X, ALU.max), segB, [tauB])
            tt("dve", cc, mx8[:, 0:1], mn, ALU.add, [tauB], [tauB])
            ts("dve", cc, cc, 0.5, None, ALU.mult, None, [tauB], [tauB])
            tt("dve", h0, mx8[:, 0:1], mn, ALU.subtract, [tauB], [tauB])
            ts("dve", h0, h0, 0.5, None, ALU.mult, None, [tauB], [tauB])
            ts("dve", hk, pow2, h0, None, ALU.mult, None, [tauB, cfB], [tauB])
            for k in range(KIT):
                ts("dve", mb[:, 0:N], a, cc, 0.0, ALU.is_ge, ALU.add, [accB, tauB, bigB], [mbB, tauB], accum=cnt)
                ts("dve", pm, cnt, TOPK - 0.5, 0.5, ALU.is_ge, ALU.subtract, [tauB], [tauB])
                stt(cc, pm, hk[:, k:k + 1], cc, ALU.mult, ALU.add, [tauB], [tauB])
            tt("dve", tau, cc, hk[:, KIT:KIT + 1], ALU.subtract, [tauB], [tauB])
        else:
            S.add("dve", lambda e: e.memset(acc[0:64, N - 64:N], -1e30), [], [accB])
            tau = tauc[:, 0:1]
        ts("dve", mb[:, 0:N], a, tau, NEG, ALU.is_lt, ALU.mult, [accB, tauB, bf("tauc"), bigB], [mbB])

    def dsa_attend(s, t4):
        qb = s * 4 + t4
        mb = mbs[qb % 2]
        qbdB, mbB = bf("qbd"), bf("mb%d" % (qb % 2))
        kcr = [KcB[i] for i in range(s + 1)]
        vcr = [VcB[i] for i in range(s + 1)]
        pool_l = (2, 3, 4, 5, 6, 7)
        units = [(st, bk) for st in range(qb + 1) for bk in range(2)]
        LAG = 2
        ubank = {}

        def logits(u):
            st, bk = units[u]
            ssl = slice(st * 128, (st + 1) * 128)
            bank = next_bank(pool_l)
            ubank[u] = bank
            mm(ps[bank][:], mb[:, ssl], ident4[:], True, False, [mbB, bf("identb")], [PB[bank]])
            for pp in range(2):
                pair = 2 * bk + pp
                mm(ps[bank][:, pp * 256:(pp + 1) * 256], Kc[:, pair, ssl], qbd[:, pair, t4, :], False, pp == 1,
                   kcr + [qbdB], [PB[bank]])
            slot = u % 4
            act(PT[:, slot, :], ps[bank][:], AF.Exp, [PB[bank]], [PTB[slot]], scale=0.125)

        def pv(u):
            st, bk = units[u]
            slot = u % 4
            for hh in range(4):
                h = 4 * bk + hh
                mm(ps[bk][:, hh * 65:(hh + 1) * 65], PT[:, slot, hh * 128:(hh + 1) * 128],
                   Vc[:, st, h * 65:(h + 1) * 65], st == 0 and hh == 0, st == qb and hh == 3,
                   [PTB[slot]] + vcr, [PB[bk]])

        for u in range(len(units) + LAG):
            if u < len(units):
                logits(u)
            if u >= LAG:
                pv(u - LAG)
        for bk in range(2):
            rden = sm(72 + 4 * bk, 76 + 4 * bk)
            dB = bf("rden%d" % bk)
            pv = ps[bk][:, 0:260].rearrange("p (h e) -> p h e", e=65)
            S.add("dve", lambda e, rden=rden, pv=pv: e.reciprocal(rden.unsqueeze(2), pv[:, :, 64:65]), [PB[bk]], [dB])
            tt("dve", omix[:, 512 + bk * 256:512 + (bk + 1) * 256].rearrange("p (h e) -> p h e", e=64),
               pv[:, :, 0:64], rden.unsqueeze(2).to_broadcast([128, 4, 64]), ALU.mult, [PB[bk], dB], [omixB])

    def omix_to_xT(t4):
        tsl = slice(t4 * 128, (t4 + 1) * 128)
        for half in range(2):
            bank = next_bank((2, 3, 4, 5, 6, 7))
            for i in range(4):
                if half == 0:
                    tr(ps[bank][:, i * 128:(i + 1) * 128], omixg[:, t4 % 2, i * 128:(i + 1) * 128],
                       [omixgB[t4 % 2]], [PB[bank]])
                else:
                    tr(ps[bank][:, i * 128:(i + 1) * 128], omix[:, 512 + i * 128:512 + (i + 1) * 128],
                       [omixB], [PB[bank]])
            cp(evac_eng(), xT[:, half * 4:half * 4 + 4, tsl], ps[bank][:].rearrange("p (a t) -> p a t", a=4),
               [PB[bank]], [xTB[half * 4 + i] for i in range(4)])

    def wout():
        for j in range(4):
            w, wB = w_next()
            wv = w.rearrange("p (c f) -> p c f", c=8)
            for t4 in range(4):
                bank = next_bank()
                for c in range(8):
                    mm(ps[bank][:, 0:256], xT[:, c, t4 * 128:(t4 + 1) * 128], wv[:, c, :], c == 0, c == 7,
                       [wB, xTB[c]], [PB[bank]])
                r = res[:, t4, j * 256:(j + 1) * 256]
                stt(r, ps[bank][:, 0:256], C_MIX, r, ALU.mult, ALU.add, [PB[bank], resB[t4]], [resB[t4]])
                if j == 3:
                    ln_stage(0, t4)
            if j == 1:
                for t4 in range(4):
                    ln_stage(-1, t4)

    def dump(dst, s):
        S.add("sp", lambda e: e.dma_start(out=dst[s * STK:(s + 1) * STK, :].rearrange("(a p) d -> p a d", p=128),
                                          in_=res[:]), reads=resB, writes=[], dma=bf("dump"))

    deferred = [None]
    for s in range(NST):
        xin = x_d[s * STK:(s + 1) * STK, :].rearrange("(a p) d -> p a d", p=128)
        if s == 0:
            S.add("sp", lambda e, xin=xin: e.dma_start(out=res[:], in_=xin), reads=[], writes=resB, dma=bf("resld"))
            load_gb(0)
            to_xT(None)
        if stop >= 1:
            def gu_hook(fc):
                if fc >= 1 and deferred[0]:
                    deferred[0].pop(0)()
            ffn(0, gu_hook=gu_hook)
        if dbg:
            dump(dbg1_d, s)
        if stop >= 2:
            barrier_bigb()
            to_xT(None)
            zero_qpads()
            projection(s, after_iw=(lambda s=s: dsa_select(s, 0)))
            gla(s, 0)
        for t4 in range(4):
            if stop >= 3:
                if t4 < 3:
                    gla(s, t4 + 1)
                    dsa_select(s, t4 + 1)
                dsa_attend(s, t4)
                omix_to_xT(t4)
            if dbg and stop >= 3:
                S.add("sp", lambda e, s=s, t4=t4: e.dma_start(
                    out=dbg2_d[s * STK + t4 * 128:s * STK + (t4 + 1) * 128, :], in_=omix[:]),
                    reads=[omixB], writes=[], dma=bf("dump2"))
        if stop >= 6:
            load_gb(1)
            wout()
            layer_norm_all(1, first=1)
        if dbg:
            dump(dbg3_d, s)
        if stop >= 7:
            load_gb(2)
            to_xT(None)
            S.add("dve", lambda e: e.memset(sm(201, 202), 0.0), [], [bf("mb0"), bf("mb1"), bf("qz"), bf("qbd"), bigB, bf("dummy")])
            def hook(dh, g, s=s):
                if s + 1 < NST and dh == 0 and 1 <= g <= 4:
                    prefetch_x_tile(s + 1, g - 1)
            last = (s + 1 >= NST)
            ffn(2, down_hook=hook, do_ln=last)

        def store_out(s=s):
            S.add("sp", lambda e: e.dma_start(
                out=out_d[s * STK:(s + 1) * STK, :].rearrange("(a p) d -> p a d", p=128), in_=res[:]),
                reads=resB, writes=[], dma=bf("outst"))

        if s + 1 >= NST or stop < 7:
            store_out()
        else:
            def make_tail(s=s, store_out=store_out):
                steps = []
                for i in range(5):
                    steps.append(lambda i=i: [ln_stage(i, t4) for t4 in range(4)])

                def fin():
                    store_out()
                    xn = x_d[(s + 1) * STK:(s + 2) * STK, :].rearrange("(a p) d -> p a d", p=128)
                    S.add("sp", lambda e: e.dma_start(out=res[:], in_=xn), reads=[], writes=resB, dma=bf("resld"))
                    load_gb(0)
                steps.append(fin)
                return steps
            deferred[0] = make_tail()
    fin = []
    for name in (["outst"] + (["dump", "dump2"] if dbg else [])):
        b = bf(name)
        fin.append(b)
    S.add("sp", lambda e: e.nop(), reads=[], writes=fin + resB + [omixB] + omixgB)
    S.emit(nc, es)
    es.close()
    return nc


def _chunk_cols(w, cols_list):
    out = np.zeros((128, 8, 256), np.float32)
    o = 0
    for cols in cols_list:
        sel = w[:, cols]
        n = sel.shape[1]
        out[:, :, o:o + n] = sel.reshape(8, 128, n).transpose(1, 0, 2)
        o += n
    return out.reshape(128, 2048)


def _build_ws(w_in, w_out, w_gu1, w_d1, w_gu2, w_d2):
    chunks = []

    def ffn_chunks(w_gu, w_d):
        for fc in range(NFC):
            chunks.append(_chunk_cols(w_gu, [np.arange(fc * 128, (fc + 1) * 128),
                                             np.arange(DFF + fc * 128, DFF + (fc + 1) * 128)]))
        for dh in range(2):
            for g in range(6):
                a = np.zeros((128, 4, 512), np.float32)
                for i in range(4):
                    fc = 4 * g + i
                    if fc < NFC:
                        a[:, i, :] = w_d[fc * 128:(fc + 1) * 128, dh * 512:(dh + 1) * 512]
                chunks.append(a.reshape(128, 2048))

    ffn_chunks(w_gu1, w_d1)
    r = np.arange
    ik = np.concatenate([r(3600, 3664), r(3600, 3664)])
    fmA = [r(1536, 1664), ik, r(0, 128), r(128, 256), r(256, 384), r(384, 512)]
    fmA += [r(3088 + p * 128, 3088 + (p + 1) * 128) for p in range(4)]
    for i in range(5):
        chunks.append(_chunk_cols(w_in, [fmA[2 * i], fmA[2 * i + 1]]))
    for cols in (r(256, 512), r(1024, 1280), r(1280, 1536), r(3664, 3672)):
        chunks.append(_chunk_cols(w_in, [cols]))
    fmB = [r(1552 + p * 128, 1552 + (p + 1) * 128) for p in range(4)]
    fmB += [r(2064 + p * 128, 2064 + (p + 1) * 128) for p in range(4)]
    for i in range(4):
        chunks.append(_chunk_cols(w_in, [fmB[2 * i], fmB[2 * i + 1]]))
    for cols in (r(512, 768), r(768, 1024), r(2576, 2832), r(2832, 3088)):
        chunks.append(_chunk_cols(w_in, [cols]))
    for j in range(4):
        chunks.append(_chunk_cols(w_out, [r(j * 256, (j + 1) * 256)]))
    ffn_chunks(w_gu2, w_d2)
    assert len(chunks) == NCH
    return np.ascontiguousarray(np.stack(chunks, 0))


def _build_cf(w_gla_a2, b_gla_a, gla_norm_g):
    cf = np.zeros((128, NCF), np.float32)
    cf[:, CF_ID:CF_ID + 128] = np.eye(128, dtype=np.float32)
    tp = np.arange(128)[:, None]
    t = np.arange(128)[None, :]
    le = (tp <= t).astype(np.float32)
    cf[:, CF_TRIN:CF_TRIN + 128] = le * np.float32(-1.0 / 16.0)
    cf[:, CF_TRIP:CF_TRIP + 128] = le * np.float32(1.0 / 16.0)
    cf[:, CF_CM:CF_CM + 128] = le
    cf[0:16, CF_WA2:CF_WA2 + 256] = w_gla_a2
    cf[:, CF_BA:CF_BA + 256] = b_gla_a[None, :]
    cf[:, CF_GN:CF_GN + 128] = gla_norm_g[None, :]
    cf[:, CF_P2:CF_P2 + KIT + 1] = (2.0 ** -np.arange(KIT + 1, dtype=np.float64)).astype(np.float32)[None, :]
    return cf


_NC_CACHE = {}


def _run(inputs, T, dbg=False, trace=False, stop=99):
    x = np.asarray(inputs["x"], np.float32)
    ws = _build_ws(np.asarray(inputs["w_in"][0], np.float32), np.asarray(inputs["w_out"][0], np.float32),
                   np.asarray(inputs["ffn1_w_gu"][0], np.float32), np.asarray(inputs["ffn1_w_down"][0], np.float32),
                   np.asarray(inputs["ffn2_w_gu"][0], np.float32), np.asarray(inputs["ffn2_w_down"][0], np.float32))
    cf = _build_cf(np.asarray(inputs["w_gla_a2"][0], np.float32), np.asarray(inputs["b_gla_a"][0], np.float32),
                   np.asarray(inputs["gla_norm_g"][0], np.float32))
    lnp = np.stack([np.asarray(inputs[k][0], np.float32) for k in
                    ("ln1_g", "ln1_b", "ln2_g", "ln2_b", "ln3_g", "ln3_b")], 0)
    lnp = np.ascontiguousarray(np.broadcast_to(lnp[:, None, :], (6, 128, D)))
    key = (T, dbg, stop)
    if key not in _NC_CACHE:
        _NC_CACHE[key] = build(T, dbg, stop)
    nc = _NC_CACHE[key]
    nb = x.shape[0]
    in_maps = [{"x": np.ascontiguousarray(x[b, :T]), "ws": ws, "cf": cf, "lnp": lnp} for b in range(nb)]
    res = run_bass_kernel_spmd(nc, in_maps, core_ids=list(range(nb)), trace=trace)
    return res


def kernel(**inputs):
    res = _run(inputs, SEQ)
    out = np.stack([np.asarray(r["out"], np.float32) for r in res.results], 0)
    return out
```

```python
from contextlib import ExitStack

import numpy as np
import concourse.bass as bass
import concourse.mybir as mybir
from concourse.alu_op_type import AluOpType as ALU
from concourse.bass_utils import run_bass_kernel_spmd

F32 = mybir.dt.float32
BF16 = mybir.dt.bfloat16
AF = mybir.ActivationFunctionType
AX = mybir.AxisListType

D = 1024
DFF = 2816
NFC = 22
SEQ = 4096
NCORES = 8
STK = 512
ALPHA = 2.0 ** 0.25
C_FFN = 0.5 / ALPHA
C_MIX = 1.0 / ALPHA
EPS_LN = 1e-5 / (ALPHA * ALPHA)
RMS_EPS = 1e-6
C_IDX = (64.0 ** -0.5) * (8.0 ** -0.5)
NEG = -30000.0
KIT = 12
TOPK = 256
RING = 3
NCH = 22 + 12 + 9 + 8 + 4 + 22 + 12

CF_ID = 0
CF_TRIN = 128
CF_TRIP = 256
CF_CM = 384
CF_WA2 = 512
CF_BA = 768
CF_GN = 1024
CF_P2 = 1152
NCF = 1152 + 32


class Buf:
    __slots__ = ("name", "w", "r", "sem", "semval", "excl")

    def __init__(self, name, excl=False):
        self.name = name
        self.excl = excl
        self.w = None
        self.r = []
        self.sem = None
        self.semval = 0


class Op:
    __slots__ = ("eng", "idx", "fn", "deps", "dma", "signal", "sigcount", "semval")

    def __init__(self, eng, idx, fn, deps, dma):
        self.eng = eng
        self.idx = idx
        self.fn = fn
        self.deps = deps
        self.dma = dma
        self.signal = False
        self.sigcount = 0
        self.semval = 0


ENGS = ("pe", "act", "dve", "pool", "sp")


class Sched:
    def __init__(self):
        self.ops = {e: [] for e in ENGS}

    def add(self, eng, fn, reads=(), writes=(), dma=None):
        deps = {}
        for b in reads:
            if b.w is not None:
                deps[id(b.w)] = b.w
            if b.excl:
                for r in b.r:
                    if r.eng != eng:
                        deps[id(r)] = r
        for b in writes:
            if b.w is not None:
                deps[id(b.w)] = b.w
            for r in b.r:
                deps[id(r)] = r
        op = Op(eng, len(self.ops[eng]), fn, list(deps.values()), dma)
        if dma is not None:
            dma.semval += 16
            op.semval = dma.semval
        for b in reads:
            b.r.append(op)
        for b in writes:
            b.w = op
            b.r = []
        self.ops[eng].append(op)
        return op

    def emit(self, nc, es):
        def plan(eng_name, mark):
            waited = {}
            out = []
            for op in self.ops[eng_name]:
                need = {}
                for d in op.deps:
                    if d.dma is not None:
                        key = ("d", id(d.dma))
                        val = d.semval
                    else:
                        if d.eng == "pe" and eng_name == "pe" and op.dma is None:
                            continue
                        key = ("e", d.eng)
                        val = d.idx
                    if waited.get(key, -1) >= val:
                        continue
                    if key not in need or need[key][0] < val:
                        need[key] = (val, d)
                for key, (val, d) in need.items():
                    waited[key] = val
                    if mark and d.dma is None:
                        d.signal = True
                out.append(list(need.values()))
            return out

        for e in ENGS:
            plan(e, True)
        for e in ENGS:
            cnt = 0
            for op in self.ops[e]:
                if op.dma is None and op.signal:
                    cnt += 1
                    op.sigcount = cnt
        plans = {e: plan(e, False) for e in ENGS}
        sems = {e: es.enter_context(nc.semaphore("sem_" + e)) for e in ENGS}
        dma_bufs = {}
        for e in ENGS:
            for op in self.ops[e]:
                if op.dma is not None and id(op.dma) not in dma_bufs:
                    dma_bufs[id(op.dma)] = op.dma
        for b in dma_bufs.values():
            b.sem = es.enter_context(nc.semaphore("dsem_" + b.name))
        block = es.enter_context(nc.Block())

        def run_engine(eng_name, handle):
            for op, needs in zip(self.ops[eng_name], plans[eng_name]):
                for val, d in needs:
                    if d.dma is not None:
                        handle.wait_ge(d.dma.sem, d.semval)
                    else:
                        handle.wait_ge(sems[d.eng], d.sigcount)
                ins = op.fn(handle)
                if op.dma is not None:
                    ins.then_inc(op.dma.sem, 16)
                elif op.signal:
                    ins.then_inc(sems[eng_name], 1)

        @block.tensor
        def _(h):
            run_engine("pe", h)

        @block.scalar
        def _(h):
            run_engine("act", h)

        @block.vector
        def _(h):
            run_engine("dve", h)

        @block.gpsimd
        def _(h):
            run_engine("pool", h)

        @block.sync
        def _(h):
            run_engine("sp", h)


def build(T, dbg=False, stop=99):
    NST = T // STK
    NQB = T // 128
    nc = bass.Bass("TRN2", target_bir_lowering=False)
    x_d = nc.dram_tensor("x", [T, D], F32, kind="ExternalInput").ap()
    ws_d = nc.dram_tensor("ws", [NCH, 128, 2048], F32, kind="ExternalInput").ap()
    cf_d = nc.dram_tensor("cf", [128, NCF], F32, kind="ExternalInput").ap()
    lnp_d = nc.dram_tensor("lnp", [6, 128, D], F32, kind="ExternalInput").ap()
    out_d = nc.dram_tensor("out", [T, D], F32, kind="ExternalOutput").ap()
    if dbg:
        dbg1_d = nc.dram_tensor("dbg1", [T, D], F32, kind="ExternalOutput").ap()
        dbg2_d = nc.dram_tensor("dbg2", [T, D], F32, kind="ExternalOutput").ap()
        dbg3_d = nc.dram_tensor("dbg3", [T, D], F32, kind="ExternalOutput").ap()

    es = ExitStack()
    S = Sched()

    def sb(name, shape, dt):
        return es.enter_context(nc.sbuf_tensor("sb_" + name, shape, dt))

    Kc = sb("Kc", [128, 4, T], BF16)
    Vc = sb("Vc", [128, NQB, 520], BF16)
    Kix = sb("Kix", [128, T], BF16)
    ring = sb("ring", [128, RING, 2048], BF16)
    res = sb("res", [128, 4, D], F32)
    xT = sb("xT", [128, 8, STK], BF16)
    bigb = sb("bigb", [128, 16384], BF16)
    acc = sb("acc", [128, 4096], F32)
    rh = sb("rh", [128, 2, 512], F32)
    PT = sb("PT", [128, 4, 512], BF16)
    gb = sb("gb", [128, 2, D], F32)
    omix = sb("omix", [128, D], F32)
    omixg = sb("omixg", [128, 2, 512], F32)
    cf = sb("cf", [128, NCF], F32)
    identb = sb("identb", [128, 128], BF16)
    ident4 = sb("ident4", [128, 512], BF16)
    qgbd = sb("qgbd", [128, 2, 4, 256], BF16)
    kgT = sb("kgT", [128, 2, STK], BF16)
    ktok = sb("ktok", [128, 4, 256], BF16)
    vg = sb("vg", [128, 4, 512], BF16)
    gsil = sb("gsil", [128, 4, 512], BF16)
    ATt = sb("ATt", [128, 512], BF16)
    Sst = sb("Sst", [128, 2, 128], F32)
    Sbt = sb("Sbt", [128, 2, 128], BF16)
    wabs = sb("wabs", [128, 4, 8], F32)
    wsgn = sb("wsgn", [128, 4, 8], F32)
    small = sb("small", [128, 256], F32)
    tauc = sb("tauc", [128, 1], F32)
    seg8 = sb("seg8", [128, 32, 8], F32)

    aT = bigb[:, 0:NFC * 512].rearrange("p (f t) -> p f t", f=NFC)
    mbs = [bigb[:, 0:4096], bigb[:, 12288:16384]]
    qz = bigb[:, 4096:8192].rearrange("p (h t) -> p h t", h=8)
    qbd = bigb[:, 8192:12288].rearrange("p (a b c) -> p a b c", a=4, b=4)
    E1 = acc[:, 0:1024].rearrange("p (a t) -> p a t", a=2)
    E2 = acc[:, 1024:2048].rearrange("p (a t) -> p a t", a=2)
    E2tok = acc[:, 2048:3072].rearrange("p (a t) -> p a t", a=4)
    gaT = acc[:, 3072:3584]
    spt = acc[:, 3584:3840]
    gbv = gb[:, 0, :].bitcast(BF16)
    Dg = gbv[:, 0:1024].rearrange("p (h q) -> p h q", h=8)
    rhb = gb[:, 1, :].bitcast(BF16).rearrange("p (a t) -> p a t", a=4)
    identf = cf[:, CF_ID:CF_ID + 128]
    trin = cf[:, CF_TRIN:CF_TRIN + 128]
    trip = cf[:, CF_TRIP:CF_TRIP + 128]
    cmask = cf[:, CF_CM:CF_CM + 128]
    wa2p = cf[:, CF_WA2:CF_WA2 + 256]
    ba_b = cf[:, CF_BA:CF_BA + 256]
    gn = cf[:, CF_GN:CF_GN + 128]
    pow2 = cf[:, CF_P2:CF_P2 + KIT + 1]

    def sm(a, b):
        return small[:, a:b]

    ps = [es.enter_context(nc.psum_tensor("ps%d" % i, [128, 512], F32)) for i in range(8)]
    PB = [Buf("ps%d" % i, excl=True) for i in range(8)]

    B = {}

    def bf(name):
        if name not in B:
            B[name] = Buf(name)
        return B[name]

    cfB = bf("cf")
    resB = [bf("res%d" % i) for i in range(4)]
    xTB = [bf("xT%d" % i) for i in range(8)]
    ringB = [bf("ring%d" % i) for i in range(RING)]
    bigB = bf("bigb")
    aTB = [bf("aT%d" % i) for i in range(NFC)]
    accB = bf("acc")
    rhB = [bf("rh0"), bf("rh1")]
    PTB = [bf("PT%d" % i) for i in range(4)]
    gbB = bf("gb")
    gbP = bf("gbP")
    rhbB = [bf("rhb%d" % i) for i in range(4)]
    omixB = bf("omix")
    omixgB = [bf("omixg0"), bf("omixg1")]
    KcB = [bf("Kc%d" % i) for i in range(NST)]
    VcB = [bf("Vc%d" % i) for i in range(NST)]
    KixB = [bf("Kix%d" % i) for i in range(NST)]

    bank_rr = [0]

    def next_bank(pool=(0, 1, 2, 3, 4, 5, 6, 7)):
        bank_rr[0] += 1
        return pool[bank_rr[0] % len(pool)]

    def mm(out, lhsT, rhs, start, stop, reads, writes, skip=False):
        S.add("pe", lambda e: e.matmul(out, lhsT, rhs, start=start, stop=stop, skip_group_check=skip), reads, writes)

    def tr(out, in_, reads, writes):
        S.add("pe", lambda e: e.transpose(out, in_, identf), list(reads) + [cfB], writes)

    def act(out, in_, func, reads, writes, scale=1.0, bias=0.0):
        S.add("act", lambda e: e.activation(out, in_, func, bias=bias, scale=scale), reads, writes)

    def cp(eng, out, in_, reads, writes):
        if eng == "act":
            S.add("act", lambda e: e.activation(out, in_, AF.Copy), reads, writes)
        else:
            S.add(eng, lambda e: e.tensor_copy(out, in_), reads, writes)

    def tt(eng, out, in0, in1, op, reads, writes):
        S.add(eng, lambda e: e.tensor_tensor(out, in0, in1, op), reads, writes)

    def ts(eng, out, in0, s1, s2, op0, op1, reads, writes, accum=None):
        if op1 is None:
            S.add(eng, lambda e: e.tensor_scalar(out, in0, s1, None, op0), reads, writes)
        else:
            S.add(eng, lambda e: e.tensor_scalar(out, in0, s1, s2, op0, op1, accum_out=accum), reads, writes)

    def stt(out, in0, scalar, in1, op0, op1, reads, writes):
        S.add("dve", lambda e: e.scalar_tensor_tensor(out, in0, scalar, in1, op0, op1), reads, writes)

    evac_rr = [0]

    def evac_eng():
        evac_rr[0] += 1
        return "act" if evac_rr[0] % 2 else "dve"

    wstate = {"issued": 0, "cur": 0}
    total_chunks = NST * NCH

    def w_issue():
        j = wstate["issued"]
        if j >= total_chunks:
            return
        slot = j % RING
        src = ws_d[j % NCH]
        S.add("pool", lambda e: e.dma_start(out=ring[:, slot, :], in_=src, max_dma_last_dim=4096),
              reads=[], writes=[ringB[slot]], dma=ringB[slot])
        wstate["issued"] = j + 1

    def w_next():
        j = wstate["cur"]
        while wstate["issued"] < min(j + RING - 1, total_chunks) or wstate["issued"] <= j:
            w_issue()
        wstate["cur"] = j + 1
        slot = j % RING
        return ring[:, slot, :], ringB[slot]

    S.add("sp", lambda e: e.dma_start(out=cf[:], in_=cf_d), reads=[], writes=[cfB], dma=cfB)
    identB_ = bf("identb")
    cp("dve", identb[:], identf, [cfB], [identB_])
    for i in range(4):
        cp("dve", ident4[:, i * 128:(i + 1) * 128], identf, [cfB], [identB_])
    S.add("dve", lambda e: e.memset(bigb[:, 4096:12288], 0.0), [], [bigB])
    S.add("dve", lambda e: e.memset(Vc[:], 1.0), [], VcB)
    S.add("dve", lambda e: e.memset(Sst[:], 0.0), [], [bf("S")])
    S.add("dve", lambda e: e.memset(Sbt[:], 0.0), [], [bf("Sb")])
    S.add("dve", lambda e: e.memset(tauc[:], -1e29), [], [bf("tauc")])
    S.add("dve", lambda e: e.memset(qgbd[:], 0.0), [], [bf("qgbd")])

    def tile_to_xT(src, src_bufs, t4):
        tsl = slice(t4 * 128, (t4 + 1) * 128)
        for half in range(2):
            bank = next_bank((0, 1, 2, 3))
            for i in range(4):
                c = half * 4 + i
                tr(ps[bank][:, i * 128:(i + 1) * 128], src[:, c * 128:(c + 1) * 128], src_bufs, [PB[bank]])
            cp(evac_eng(), xT[:, half * 4:half * 4 + 4, tsl], ps[bank][:].rearrange("p (a t) -> p a t", a=4),
               [PB[bank]], [xTB[half * 4 + i] for i in range(4)])

    def to_xT(_unused):
        for t4 in range(4):
            tile_to_xT(res[:, t4, :], [resB[t4]], t4)

    stg = [omix[:, :], rh[:, :, :].rearrange("p a t -> p (a t)")]
    stgB = [[omixB], [rhB[0], rhB[1]]]

    def prefetch_x_tile(s_next, t4):
        k = t4 % 2
        src = x_d[s_next * STK + t4 * 128:s_next * STK + (t4 + 1) * 128, :]
        S.add("sp", lambda e: e.dma_start(out=stg[k], in_=src), reads=[], writes=stgB[k], dma=bf("stgd%d" % k))
        tile_to_xT(stg[k], stgB[k], t4)

    ln_hoisted = set()

    def ln_stage(i, t4):
        st6 = sm(t4 * 16, t4 * 16 + 12)
        mv = sm(t4 * 16 + 12, t4 * 16 + 14)
        lnv = sm(t4 * 16 + 14, t4 * 16 + 15)
        rstd = sm(t4 * 16 + 15, t4 * 16 + 16)
        nmr = sm(160 + t4, 161 + t4)
        sB = bf("lnstat%d" % t4)
        r = res[:, t4, :]
        if i == -1:
            S.add("dve", lambda e: e.bn_stats(st6[:, 0:6], res[:, t4, 0:512]), [resB[t4]], [sB])
            ln_hoisted.add(t4)
        elif i == 0:
            if t4 in ln_hoisted:
                ln_hoisted.discard(t4)
            else:
                S.add("dve", lambda e: e.bn_stats(st6[:, 0:6], res[:, t4, 0:512]), [resB[t4]], [sB])
            S.add("dve", lambda e: e.bn_stats(st6[:, 6:12], res[:, t4, 512:1024]), [resB[t4]], [sB])
            S.add("dve", lambda e: e.bn_aggr(mv, st6), [sB], [sB])
            ts("dve", lnv, mv[:, 1:2], EPS_LN, None, ALU.add, None, [sB], [sB])
        elif i == 1:
            act(lnv, lnv, AF.Ln, [sB], [sB])
            act(rstd, lnv, AF.Exp, [sB], [sB], scale=-0.5)
        elif i == 2:
            stt(nmr, mv[:, 0:1], -1.0, rstd, ALU.mult, ALU.mult, [sB], [sB])
        elif i == 3:
            act(r, r, AF.Identity, [sB, resB[t4]], [resB[t4]], scale=rstd, bias=nmr)
        else:
            tt("dve", r, r, gb[:, 0, :], ALU.mult, [resB[t4], gbB, gbP], [resB[t4]])
            tt("dve", r, r, gb[:, 1, :], ALU.add, [resB[t4], gbB, gbP], [resB[t4]])

    ln_xT = [False]

    def layer_norm_all(k, first=0):
        last = 5 if ln_xT[0] else 4
        for step in range(first, last + 1 + 3):
            for t4 in range(4):
                i = step - t4
                if first <= i <= 4:
                    ln_stage(i, t4)
                elif i == 5 and ln_xT[0]:
                    tile_to_xT(res[:, t4, :], [resB[t4]], t4)

    def load_gb(k):
        for j in range(2):
            src = lnp_d[2 * k + j]
            S.add("sp", (lambda e, j=j, src=src: e.dma_start(out=gb[:, j, :], in_=src)), reads=[], writes=[gbB, gbP], dma=gbB)

    def ffn(k, down_hook=None, gu_hook=None, do_ln=True):
        for fc in range(NFC):
            w, wB = w_next()
            wv = w.rearrange("p (c f) -> p c f", c=8)
            bg = (0, 2)[fc % 2]
            bu = (1, 3)[fc % 2]
            for c in range(8):
                mm(ps[bg][:], wv[:, c, 0:128], xT[:, c, :], c == 0, c == 7, [wB, xTB[c]], [PB[bg]])
            for c in range(8):
                mm(ps[bu][:], wv[:, c, 128:256], xT[:, c, :], c == 0, c == 7, [wB, xTB[c]], [PB[bu]])
            sl = fc % 2
            act(rh[:, sl, :], ps[bg][:], AF.Silu, [PB[bg]], [rhB[sl]])
            tt("dve", aT[:, fc, :], rh[:, sl, :], ps[bu][:], ALU.mult, [rhB[sl], PB[bu], bigB], [aTB[fc]])
            if gu_hook is not None:
                gu_hook(fc)
        for dh in range(2):
            for g in range(6):
                w, wB = w_next()
                wv = w.rearrange("p (i f) -> p i f", i=4)
                for i in range(4):
                    fc = 4 * g + i
                    if fc >= NFC:
                        break
                    for t4 in range(4):
                        mm(ps[4 + t4][:], aT[:, fc, t4 * 128:(t4 + 1) * 128], wv[:, i, :], fc == 0, fc == NFC - 1,
                           [wB, aTB[fc]], [PB[4 + t4]])
                if down_hook is not None:
                    down_hook(dh, g)
            for t4 in range(4):
                r = res[:, t4, dh * 512:(dh + 1) * 512]
                stt(r, ps[4 + t4][:], C_FFN, r, ALU.mult, ALU.add, [PB[4 + t4], resB[t4]], [resB[t4]])
                if dh == 1 and do_ln:
                    ln_stage(0, t4)
            if dh == 0:
                for t4 in range(4):
                    ln_stage(-1, t4)
        if do_ln:
            layer_norm_all(k, first=1)

    def barrier_bigb(reads_extra=()):
        d = sm(200, 201)
        S.add("dve", lambda e: e.memset(d, 0.0), [], list(aTB) + [bigB, gbP, bf("dummy")])

    def zero_qpads():
        S.add("dve", lambda e: e.memset(bigb[:, 4096:12288], 0.0), [bigB], [bf("qz"), bf("qbd")])

    def projection(s, after_iw=None):
        tok0 = s * STK
        qgB, kgB, qbdB, qzB = bf("qgbd"), bf("kgT"), bf("qbd"), bf("qz")
        def fm_chunk(names):
            w, wB = w_next()
            wv = w.rearrange("p (c f) -> p c f", c=8)
            for half in range(2):
                name = names[half]
                bank = next_bank()
                for c in range(8):
                    mm(ps[bank][:], wv[:, c, half * 128:(half + 1) * 128], xT[:, c, :], c == 0, c == 7,
                       [wB, xTB[c]], [PB[bank]])
                pb = PB[bank]
                pv = ps[bank]
                if name == "ga":
                    cp("act", gaT, pv[:], [pb], [accB])
                    for t4 in range(4):
                        b2 = next_bank()
                        mm(ps[b2][:, 0:256], gaT[:, t4 * 128:(t4 + 1) * 128], wa2p, True, True, [accB, cfB], [PB[b2]])
                        tt("dve", spt, ps[b2][:, 0:256], ba_b, ALU.add, [PB[b2], cfB], [accB])
                        act(spt, spt, AF.Exp, [accB], [accB], scale=-1.0)
                        ts("dve", spt, spt, 1.0, None, ALU.add, None, [accB], [accB])
                        act(spt, spt, AF.Ln, [accB], [accB])
                        b3 = next_bank()
                        for p in range(2):
                            mm(ps[b3][:, p * 128:(p + 1) * 128], spt[:, p * 128:(p + 1) * 128], trin, p == 0, p == 1,
                               [accB, cfB], [PB[b3]])
                        for p in range(2):
                            act(sm(120 + p * 4 + t4, 121 + p * 4 + t4), ps[b3][:, p * 128 + 127:p * 128 + 128], AF.Exp,
                                [PB[b3]], [bf("ebl")])
                            act(E1[:, p, t4 * 128:(t4 + 1) * 128], ps[b3][:, p * 128:(p + 1) * 128], AF.Exp,
                                [PB[b3]], [accB])
                            act(E2[:, p, t4 * 128:(t4 + 1) * 128], ps[b3][:, p * 128:(p + 1) * 128], AF.Exp,
                                [PB[b3]], [accB], scale=-1.0)
                        b4 = next_bank()
                        mm(ps[b4][:, 0:256], trip, spt, True, True, [accB, cfB], [PB[b4]])
                        act(E2tok[:, t4, :], ps[b4][:, 0:256], AF.Exp, [PB[b4]], [accB])
                elif name == "ik":
                    cp(evac_eng(), Kix[:, tok0:tok0 + STK], pv[:], [pb], [KixB[s]])
                elif name.startswith("gq"):
                    p = int(name[2])
                    for hh in range(2):
                        rows = slice(hh * 64, (hh + 1) * 64)
                        stt(qgbd[rows, p, :, hh * 128:(hh + 1) * 128],
                            pv[rows, :].rearrange("p (a t) -> p a t", a=4), 0.125,
                            E1[rows, p, :].rearrange("p (a t) -> p a t", a=4), ALU.mult, ALU.mult,
                            [pb, accB], [qgB])
                elif name.startswith("gk"):
                    p = int(name[2])
                    tt("dve", kgT[:, p, :], pv[:], E2[:, p, :], ALU.mult, [pb, accB], [kgB])
                elif name.startswith("dq"):
                    p = int(name[2])
                    for hh in range(2):
                        rows = slice(hh * 64, (hh + 1) * 64)
                        cp("act", qbd[rows, p, :, hh * 128:(hh + 1) * 128],
                           pv[rows, :].rearrange("p (a t) -> p a t", a=4), [pb, bigB], [qbdB])
                elif name.startswith("dk"):
                    p = int(name[2])
                    cp("act", Kc[:, p, tok0:tok0 + STK], pv[:], [pb], [KcB[s]])
                elif name.startswith("iq"):
                    p = int(name[2])
                    for hh in range(2):
                        rows = slice(hh * 64, (hh + 1) * 64)
                        cp(evac_eng(), qz[rows, 2 * p + hh, :], pv[rows, :], [pb, bigB], [qzB])
        ktB, vgB, gsB, wB_ = bf("ktok"), bf("vg"), bf("gsil"), bf("wabs")

        def tm_chunk(name):
            w, wB = w_next()
            wv = w.rearrange("p (c f) -> p c f", c=8)
            ncol = 8 if name == "iw" else 256
            for t4 in range(4):
                bank = next_bank()
                for c in range(8):
                    mm(ps[bank][:, 0:ncol], xT[:, c, t4 * 128:(t4 + 1) * 128], wv[:, c, 0:ncol], c == 0, c == 7,
                       [wB, xTB[c]], [PB[bank]])
                pb = PB[bank]
                pv = ps[bank][:, 0:ncol]
                if name == "gk":
                    tt("dve", ktok[:, t4, :], pv, E2tok[:, t4, :], ALU.mult, [pb, accB], [ktB])
                elif name.startswith("gv"):
                    i = int(name[2])
                    cp("act", vg[:, t4, i * 256:(i + 1) * 256], pv, [pb], [vgB])
                elif name.startswith("gg"):
                    i = int(name[2])
                    sl = t4 % 2
                    act(rh[:, sl, 0:256], pv, AF.Silu, [pb], [rhB[sl]])
                    tt("dve", gsil[:, t4, i * 256:(i + 1) * 256].rearrange("p (h v) -> p h v", h=2),
                       rh[:, sl, 0:256].rearrange("p (h v) -> p h v", h=2),
                       gn.unsqueeze(1).to_broadcast([128, 2, 128]), ALU.mult, [rhB[sl], cfB], [gsB])
                elif name.startswith("dv"):
                    i = int(name[2])
                    dst = Vc[:, s * 4 + t4, :].rearrange("p (h e) -> p h e", e=65)[:, 4 * i:4 * i + 4, 0:64]
                    cp("act", dst, pv.rearrange("p (h e) -> p h e", e=64), [pb], [VcB[s]])
                elif name == "iw":
                    ts("dve", wsgn[:, t4, :], pv, 0.0, 2.0, ALU.is_ge, ALU.mult, [pb], [wB_])
                    ts("dve", wsgn[:, t4, :], wsgn[:, t4, :], -1.0, None, ALU.add, None, [wB_], [wB_])
                    stt(wabs[:, t4, :], pv, C_IDX, wsgn[:, t4, :], ALU.mult, ALU.mult, [pb, wB_], [wB_])

        for names in (("ga", "ik"), ("gq0", "gq1"), ("gk0", "gk1"), ("iq0", "iq1"), ("iq2", "iq3")):
            fm_chunk(names)
        tm_chunk("gk")
        tm_chunk("gg0")
        tm_chunk("gg1")
        tm_chunk("iw")
        if after_iw is not None:
            after_iw()
        for names in (("dq0", "dq1"), ("dq2", "dq3"), ("dk0", "dk1"), ("dk2", "dk3")):
            fm_chunk(names)
        for name in ("gv0", "gv1", "dv0", "dv1"):
            tm_chunk(name)

    def gla(s, t4):
        qgB, kgB, ktB, vgB, gsB = bf("qgbd"), bf("kgT"), bf("ktok"), bf("vg"), bf("gsil")
        SB_, SbB, ATB = bf("S"), bf("Sb"), bf("AT")
        tsl = slice(t4 * 128, (t4 + 1) * 128)
        bA = next_bank()
        for p in range(2):
            mm(ps[bA][:, p * 256:(p + 1) * 256], kgT[:, p, tsl], qgbd[:, p, t4, :], p == 0, p == 1,
               [kgB, qgB], [PB[bA]])
        tt("dve", ATt[:].rearrange("p (h t) -> p h t", h=4), ps[bA][:].rearrange("p (h t) -> p h t", h=4),
           cmask.unsqueeze(1).to_broadcast([128, 4, 128]), ALU.mult, [PB[bA], cfB], [ATB])
        bo = next_bank()
        for h in range(4):
            p = h // 2
            mm(ps[bo][:, h * 128:(h + 1) * 128], ATt[:, h * 128:(h + 1) * 128], vg[:, t4, h * 128:(h + 1) * 128],
               h == 0, False, [ATB, vgB], [PB[bo]])
            mm(ps[bo][:, h * 128:(h + 1) * 128], qgbd[:, p, t4, (h % 2) * 128:(h % 2 + 1) * 128], Sbt[:, p, :],
               False, h == 3, [qgB, SbB], [PB[bo]])
        bS = next_bank()
        for p in range(2):
            mm(ps[bS][:, p * 256:(p + 1) * 256], ktok[:, t4, p * 128:(p + 1) * 128], vg[:, t4, p * 256:(p + 1) * 256],
               p == 0, p == 1, [ktB, vgB], [PB[bS]])
        for p in range(2):
            for hh in range(2):
                rows = slice(hh * 64, (hh + 1) * 64)
                tt("dve", Sst[rows, p, :], Sst[rows, p, :], ps[bS][rows, p * 256 + hh * 128:p * 256 + (hh + 1) * 128],
                   ALU.add, [SB_, PB[bS]], [SB_])
                ts("dve", Sst[rows, p, :], Sst[rows, p, :], small[rows, 120 + p * 4 + t4:121 + p * 4 + t4], None,
                   ALU.mult, None, [SB_, bf("ebl")], [SB_])
            cp("act", Sbt[:, p, :], Sst[:, p, :], [SB_], [SbB])
        ot = rh[:, 0, :]
        sq = rh[:, 1, :]
        ssq = sm(64, 68)
        rs = sm(68, 72)
        gB = bf("glastat")
        cp("act", ot, ps[bo][:], [PB[bo]], [rhB[0]])
        tt("dve", sq, ot, ot, ALU.mult, [rhB[0]], [rhB[1]])
        S.add("dve", lambda e: e.tensor_reduce(ssq, sq.rearrange("p (h v) -> p h v", h=4), AX.X, ALU.add),
              [rhB[1]], [gB])
        ts("dve", ssq, ssq, 1.0 / 128.0, RMS_EPS, ALU.mult, ALU.add, [gB], [gB])
        act(ssq, ssq, AF.Ln, [gB], [gB])
        act(rs, ssq, AF.Exp, [gB], [gB], scale=-0.5)
        tt("dve", ot.rearrange("p (h v) -> p h v", h=4), ot.rearrange("p (h v) -> p h v", h=4),
           rs.unsqueeze(2).to_broadcast([128, 4, 128]), ALU.mult, [rhB[0], gB], [rhB[0]])
        tt("dve", omixg[:, t4 % 2, :], ot, gsil[:, t4, :], ALU.mult, [rhB[0], gsB], [omixgB[t4 % 2]])

    def dsa_select(s, t4):
        qb = s * 4 + t4
        N = (qb + 1) * 128
        nkt = (N + 511) // 512
        tsl = slice(t4 * 128, (t4 + 1) * 128)
        mb = mbs[qb % 2]
        qzB, wB_, mbB, tauB = bf("qz"), bf("wabs"), bf("mb%d" % (qb % 2)), bf("tau")
        kixr = [KixB[i] for i in range(s + 1)]
        DB = bf("Dg")
        for h in range(8):
            ts("dve", Dg[:, h, :], identb[:], wsgn[:, t4, h:h + 1], None, ALU.mult, None,
               [bf("identb"), wB_, gbP], [DB])
        LAG = 3
        steps = [(kt, h) for kt in range(nkt) for h in range(8)]
        info = {}

        def score(i):
            kt, h = steps[i]
            ncol = min(512, N - kt * 512)
            bank = next_bank((2, 3, 4, 5, 6, 7))
            sl = i % 4
            mm(ps[bank][:, 0:ncol], qz[:, h, tsl], Kix[:, kt * 512:kt * 512 + ncol], True, True, [qzB] + kixr,
               [PB[bank]])
            if i % 2 == 0:
                act(rhb[:, sl, 0:ncol], ps[bank][:, 0:ncol], AF.Relu, [PB[bank], wB_, gbP], [rhbB[sl]],
                    scale=wabs[:, t4, h:h + 1])
            else:
                ts("dve", rhb[:, sl, 0:ncol], ps[bank][:, 0:ncol], wabs[:, t4, h:h + 1], 0.0, ALU.mult, ALU.max,
                   [PB[bank], wB_, gbP], [rhbB[sl]])

        def hsum(i):
            kt, h = steps[i]
            ncol = min(512, N - kt * 512)
            bsum = (0, 1)[kt % 2]
            sl = i % 4
            mm(ps[bsum][:, 0:ncol], Dg[:, h, :], rhb[:, sl, 0:ncol], h == 0, h == 7, [DB, rhbB[sl], gbP], [PB[bsum]])
            if h == 7:
                cp("act", acc[:, kt * 512:kt * 512 + ncol], ps[bsum][:, 0:ncol], [PB[bsum]], [accB])

        for i in range(len(steps) + LAG):
            if i < len(steps):
                score(i)
            if i >= LAG:
                hsum(i - LAG)
        a = acc[:, 0:N]
        if qb >= 2:
            mn = sm(80, 81)
            mx8 = sm(88, 96)
            cc = sm(81, 82)
            h0 = sm(82, 83)
            cnt = sm(83, 84)
            pm = sm(84, 85)
            hk = sm(96, 96 + KIT + 1)
            tau = sm(85, 86)
            S.add("dve", lambda e: e.memset(acc[0:64, N - 64:N], -1e30), [], [accB])
            segv = a.rearrange("p (j g) -> p g j", g=32)
            for g in range(32):
                S.add("dve", lambda e, g=g: e.max(seg8[:, g, :], segv[:, g, :]), [accB], [bf("seg%d" % g)])
            segB = [bf("seg%d" % g) for g in range(32)]
            S.add("dve", lambda e: e.tensor_reduce(mn, seg8[:, :, 7], AX.X, ALU.min), segB, [tauB])
            S.add("dve", lambda e: e.tensor_reduce(mx8[:, 0:1], seg8[:, :, 7], AX.X, ALU.max), segB, [tauB])
            tt("dve", cc, mx8[:, 0:1], mn, ALU.add, [tauB], [tauB])
            ts("dve", cc, cc, 0.5, None, ALU.mult, None, [tauB], [tauB])
            tt("dve", h0, mx8[:, 0:1], mn, ALU.subtract, [tauB], [tauB])
            ts("dve", h0, h0, 0.5, None, ALU.mult, None, [tauB], [tauB])
            ts("dve", hk, pow2, h0, None, ALU.mult, None, [tauB, cfB], [tauB])
            for k in range(KIT):
                ts("dve", mb[:, 0:N], a, cc, 0.0, ALU.is_ge, ALU.add, [accB, tauB, bigB], [mbB, tauB], accum=cnt)
                ts("dve", pm, cnt, TOPK - 0.5, 0.5, ALU.is_ge, ALU.subtract, [tauB], [tauB])
                stt(cc, pm, hk[:, k:k + 1], cc, ALU.mult, ALU.add, [tauB], [tauB])
            tt("dve", tau, cc, hk[:, KIT:KIT + 1], ALU.subtract, [tauB], [tauB])
        else:
            S.add("dve", lambda e: e.memset(acc[0:64, N - 64:N], -1e30), [], [accB])
            tau = tauc[:, 0:1]
        ts("dve", mb[:, 0:N], a, tau, NEG, ALU.is_lt, ALU.mult, [accB, tauB, bf("tauc"), bigB], [mbB])

    def dsa_attend(s, t4):
        qb = s * 4 + t4
        mb = mbs[qb % 2]
        qbdB, mbB = bf("qbd"), bf("mb%d" % (qb % 2))
        kcr = [KcB[i] for i in range(s + 1)]
        vcr = [VcB[i] for i in range(s + 1)]
        pool_l = (2, 3, 4, 5, 6, 7)
        units = [(st, bk) for st in range(qb + 1) for bk in range(2)]
        LAG = 2
        ubank = {}

        def logits(u):
            st, bk = units[u]
            ssl = slice(st * 128, (st + 1) * 128)
            bank = next_bank(pool_l)
            ubank[u] = bank
            mm(ps[bank][:], mb[:, ssl], ident4[:], True, False, [mbB, bf("identb")], [PB[bank]])
            for pp in range(2):
                pair = 2 * bk + pp
                mm(ps[bank][:, pp * 256:(pp + 1) * 256], Kc[:, pair, ssl], qbd[:, pair, t4, :], False, pp == 1,
                   kcr + [qbdB], [PB[bank]])
            slot = u % 4
            act(PT[:, slot, :], ps[bank][:], AF.Exp, [PB[bank]], [PTB[slot]], scale=0.125)

        def pv(u):
            st, bk = units[u]
            slot = u % 4
            for hh in range(4):
                h = 4 * bk + hh
                mm(ps[bk][:, hh * 65:(hh + 1) * 65], PT[:, slot, hh * 128:(hh + 1) * 128],
                   Vc[:, st, h * 65:(h + 1) * 65], st == 0 and hh == 0, st == qb and hh == 3,
                   [PTB[slot]] + vcr, [PB[bk]])

        for u in range(len(units) + LAG):
            if u < len(units):
                logits(u)
            if u >= LAG:
                pv(u - LAG)
        for bk in range(2):
            rden = sm(72 + 4 * bk, 76 + 4 * bk)
            dB = bf("rden%d" % bk)
            pv = ps[bk][:, 0:260].rearrange("p (h e) -> p h e", e=65)
            S.add("dve", lambda e, rden=rden, pv=pv: e.reciprocal(rden.unsqueeze(2), pv[:, :, 64:65]), [PB[bk]], [dB])
            tt("dve", omix[:, 512 + bk * 256:512 + (bk + 1) * 256].rearrange("p (h e) -> p h e", e=64),
               pv[:, :, 0:64], rden.unsqueeze(2).to_broadcast([128, 4, 64]), ALU.mult, [PB[bk], dB], [omixB])

    def omix_to_xT(t4):
        tsl = slice(t4 * 128, (t4 + 1) * 128)
        for half in range(2):
            bank = next_bank((2, 3, 4, 5, 6, 7))
            for i in range(4):
                if half == 0:
                    tr(ps[bank][:, i * 128:(i + 1) * 128], omixg[:, t4 % 2, i * 128:(i + 1) * 128],
                       [omixgB[t4 % 2]], [PB[bank]])
                else:
                    tr(ps[bank][:, i * 128:(i + 1) * 128], omix[:, 512 + i * 128:512 + (i + 1) * 128],
                       [omixB], [PB[bank]])
            cp(evac_eng(), xT[:, half * 4:half * 4 + 4, tsl], ps[bank][:].rearrange("p (a t) -> p a t", a=4),
               [PB[bank]], [xTB[half * 4 + i] for i in range(4)])

    def wout():
        for j in range(4):
            w, wB = w_next()
            wv = w.rearrange("p (c f) -> p c f", c=8)
            for t4 in range(4):
                bank = next_bank()
                for c in range(8):
                    mm(ps[bank][:, 0:256], xT[:, c, t4 * 128:(t4 + 1) * 128], wv[:, c, :], c == 0, c == 7,
                       [wB, xTB[c]], [PB[bank]])
                r = res[:, t4, j * 256:(j + 1) * 256]
                stt(r, ps[bank][:, 0:256], C_MIX, r, ALU.mult, ALU.add, [PB[bank], resB[t4]], [resB[t4]])
                if j == 3:
                    ln_stage(0, t4)
            if j == 1:
                for t4 in range(4):
                    ln_stage(-1, t4)

    def dump(dst, s):
        S.add("sp", lambda e: e.dma_start(out=dst[s * STK:(s + 1) * STK, :].rearrange("(a p) d -> p a d", p=128),
                                          in_=res[:]), reads=resB, writes=[], dma=bf("dump"))

    deferred = [None]
    for s in range(NST):
        xin = x_d[s * STK:(s + 1) * STK, :].rearrange("(a p) d -> p a d", p=128)
        if s == 0:
            S.add("sp", lambda e, xin=xin: e.dma_start(out=res[:], in_=xin), reads=[], writes=resB, dma=bf("resld"))
            load_gb(0)
            to_xT(None)
        if stop >= 1:
            def gu_hook(fc):
                if fc >= 1 and deferred[0]:
                    deferred[0].pop(0)()
            ln_xT[0] = stop >= 2
            ffn(0, gu_hook=gu_hook)
            ln_xT[0] = False
        if dbg:
            dump(dbg1_d, s)
        if stop >= 2:
            barrier_bigb()
            zero_qpads()
            projection(s, after_iw=(lambda s=s: dsa_select(s, 0)))
            gla(s, 0)
        for t4 in range(4):
            if stop >= 3:
                if t4 < 3:
                    gla(s, t4 + 1)
                    dsa_select(s, t4 + 1)
                dsa_attend(s, t4)
                omix_to_xT(t4)
            if dbg and stop >= 3:
                S.add("sp", lambda e, s=s, t4=t4: e.dma_start(
                    out=dbg2_d[s * STK + t4 * 128:s * STK + (t4 + 1) * 128, :], in_=omix[:]),
                    reads=[omixB], writes=[], dma=bf("dump2"))
        if stop >= 6:
            load_gb(1)
            wout()
            ln_xT[0] = True
            layer_norm_all(1, first=1)
            ln_xT[0] = False
        if dbg:
            dump(dbg3_d, s)
        if stop >= 7:
            load_gb(2)
            S.add("dve", lambda e: e.memset(sm(201, 202), 0.0), [], [bf("mb0"), bf("mb1"), bf("qz"), bf("qbd"), bigB, bf("dummy")])
            def hook(dh, g, s=s):
                if s + 1 < NST and dh == 0 and 1 <= g <= 4:
                    prefetch_x_tile(s + 1, g - 1)
            last = (s + 1 >= NST)
            ffn(2, down_hook=hook, do_ln=last)

        def store_out(s=s):
            S.add("sp", lambda e: e.dma_start(
                out=out_d[s * STK:(s + 1) * STK, :].rearrange("(a p) d -> p a d", p=128), in_=res[:]),
                reads=resB, writes=[], dma=bf("outst"))

        if s + 1 >= NST or stop < 7:
            store_out()
        else:
            def make_tail(s=s, store_out=store_out):
                steps = []
                for i in range(5):
                    steps.append(lambda i=i: [ln_stage(i, t4) for t4 in range(4)])

                def fin():
                    store_out()
                    xn = x_d[(s + 1) * STK:(s + 2) * STK, :].rearrange("(a p) d -> p a d", p=128)
                    S.add("sp", lambda e: e.dma_start(out=res[:], in_=xn), reads=[], writes=resB, dma=bf("resld"))
                    load_gb(0)
                steps.append(fin)
                return steps
            deferred[0] = make_tail()
    fin = []
    for name in (["outst"] + (["dump", "dump2"] if dbg else [])):
        b = bf(name)
        fin.append(b)
    S.add("sp", lambda e: e.nop(), reads=[], writes=fin + resB + [omixB] + omixgB)
    S.emit(nc, es)
    es.close()
    return nc


def _chunk_cols(w, cols_list):
    out = np.zeros((128, 8, 256), np.float32)
    o = 0
    for cols in cols_list:
        sel = w[:, cols]
        n = sel.shape[1]
        out[:, :, o:o + n] = sel.reshape(8, 128, n).transpose(1, 0, 2)
        o += n
    return out.reshape(128, 2048)


def _build_ws(w_in, w_out, w_gu1, w_d1, w_gu2, w_d2):
    chunks = []

    def ffn_chunks(w_gu, w_d):
        for fc in range(NFC):
            chunks.append(_chunk_cols(w_gu, [np.arange(fc * 128, (fc + 1) * 128),
                                             np.arange(DFF + fc * 128, DFF + (fc + 1) * 128)]))
        for dh in range(2):
            for g in range(6):
                a = np.zeros((128, 4, 512), np.float32)
                for i in range(4):
                    fc = 4 * g + i
                    if fc < NFC:
                        a[:, i, :] = w_d[fc * 128:(fc + 1) * 128, dh * 512:(dh + 1) * 512]
                chunks.append(a.reshape(128, 2048))

    ffn_chunks(w_gu1, w_d1)
    r = np.arange
    ik = np.concatenate([r(3600, 3664), r(3600, 3664)])
    fmA = [r(1536, 1664), ik, r(0, 128), r(128, 256), r(256, 384), r(384, 512)]
    fmA += [r(3088 + p * 128, 3088 + (p + 1) * 128) for p in range(4)]
    for i in range(5):
        chunks.append(_chunk_cols(w_in, [fmA[2 * i], fmA[2 * i + 1]]))
    for cols in (r(256, 512), r(1024, 1280), r(1280, 1536), r(3664, 3672)):
        chunks.append(_chunk_cols(w_in, [cols]))
    fmB = [r(1552 + p * 128, 1552 + (p + 1) * 128) for p in range(4)]
    fmB += [r(2064 + p * 128, 2064 + (p + 1) * 128) for p in range(4)]
    for i in range(4):
        chunks.append(_chunk_cols(w_in, [fmB[2 * i], fmB[2 * i + 1]]))
    for cols in (r(512, 768), r(768, 1024), r(2576, 2832), r(2832, 3088)):
        chunks.append(_chunk_cols(w_in, [cols]))
    for j in range(4):
        chunks.append(_chunk_cols(w_out, [r(j * 256, (j + 1) * 256)]))
    ffn_chunks(w_gu2, w_d2)
    assert len(chunks) == NCH
    return np.ascontiguousarray(np.stack(chunks, 0))


def _build_cf(w_gla_a2, b_gla_a, gla_norm_g):
    cf = np.zeros((128, NCF), np.float32)
    cf[:, CF_ID:CF_ID + 128] = np.eye(128, dtype=np.float32)
    tp = np.arange(128)[:, None]
    t = np.arange(128)[None, :]
    le = (tp <= t).astype(np.float32)
    cf[:, CF_TRIN:CF_TRIN + 128] = le * np.float32(-1.0 / 16.0)
    cf[:, CF_TRIP:CF_TRIP + 128] = le * np.float32(1.0 / 16.0)
    cf[:, CF_CM:CF_CM + 128] = le
    cf[0:16, CF_WA2:CF_WA2 + 256] = w_gla_a2
    cf[:, CF_BA:CF_BA + 256] = b_gla_a[None, :]
    cf[:, CF_GN:CF_GN + 128] = gla_norm_g[None, :]
    cf[:, CF_P2:CF_P2 + KIT + 1] = (2.0 ** -np.arange(KIT + 1, dtype=np.float64)).astype(np.float32)[None, :]
    return cf


_NC_CACHE = {}


def _run(inputs, T, dbg=False, trace=False, stop=99):
    x = np.asarray(inputs["x"], np.float32)
    ws = _build_ws(np.asarray(inputs["w_in"][0], np.float32), np.asarray(inputs["w_out"][0], np.float32),
                   np.asarray(inputs["ffn1_w_gu"][0], np.float32), np.asarray(inputs["ffn1_w_down"][0], np.float32),
                   np.asarray(inputs["ffn2_w_gu"][0], np.float32), np.asarray(inputs["ffn2_w_down"][0], np.float32))
    cf = _build_cf(np.asarray(inputs["w_gla_a2"][0], np.float32), np.asarray(inputs["b_gla_a"][0], np.float32),
                   np.asarray(inputs["gla_norm_g"][0], np.float32))
    lnp = np.stack([np.asarray(inputs[k][0], np.float32) for k in
                    ("ln1_g", "ln1_b", "ln2_g", "ln2_b", "ln3_g", "ln3_b")], 0)
    lnp = np.ascontiguousarray(np.broadcast_to(lnp[:, None, :], (6, 128, D)))
    key = (T, dbg, stop)
    if key not in _NC_CACHE:
        _NC_CACHE[key] = build(T, dbg, stop)
    nc = _NC_CACHE[key]
    nb = x.shape[0]
    in_maps = [{"x": np.ascontiguousarray(x[b, :T]), "ws": ws, "cf": cf, "lnp": lnp} for b in range(nb)]
    res = run_bass_kernel_spmd(nc, in_maps, core_ids=list(range(nb)), trace=trace)
    return res


def kernel(**inputs):
    res = _run(inputs, SEQ)
    out = np.stack([np.asarray(r["out"], np.float32) for r in res.results], 0)
    return out
```

```python
from contextlib import ExitStack

import numpy as np
import concourse.bass as bass
import concourse.mybir as mybir
from concourse.alu_op_type import AluOpType as ALU
from concourse.bass_utils import run_bass_kernel_spmd

F32 = mybir.dt.float32
BF16 = mybir.dt.bfloat16
AF = mybir.ActivationFunctionType
AX = mybir.AxisListType

D = 1024
DFF = 2816
NFC = 22
SEQ = 4096
NCORES = 8
STK = 512
ALPHA = 2.0 ** 0.25
C_FFN = 0.5 / ALPHA
C_MIX = 1.0 / ALPHA
EPS_LN = 1e-5 / (ALPHA * ALPHA)
RMS_EPS = 1e-6
C_IDX = (64.0 ** -0.5) * (8.0 ** -0.5)
NEG = -30000.0
KIT = 12
TOPK = 256
RING = 3
NCH = 22 + 12 + 9 + 8 + 4 + 22 + 12

CF_ID = 0
CF_TRIN = 128
CF_TRIP = 256
CF_CM = 384
CF_WA2 = 512
CF_BA = 768
CF_GN = 1024
CF_P2 = 1152
NCF = 1152 + 32


class Buf:
    __slots__ = ("name", "w", "r", "sem", "semval", "excl")

    def __init__(self, name, excl=False):
        self.name = name
        self.excl = excl
        self.w = None
        self.r = []
        self.sem = None
        self.semval = 0


class Op:
    __slots__ = ("eng", "idx", "fn", "deps", "dma", "signal", "sigcount", "semval")

    def __init__(self, eng, idx, fn, deps, dma):
        self.eng = eng
        self.idx = idx
        self.fn = fn
        self.deps = deps
        self.dma = dma
        self.signal = False
        self.sigcount = 0
        self.semval = 0


ENGS = ("pe", "act", "dve", "pool", "sp")


class Sched:
    def __init__(self):
        self.ops = {e: [] for e in ENGS}

    def add(self, eng, fn, reads=(), writes=(), dma=None):
        deps = {}
        for b in reads:
            if b.w is not None:
                deps[id(b.w)] = b.w
            if b.excl:
                for r in b.r:
                    if r.eng != eng:
                        deps[id(r)] = r
        for b in writes:
            if b.w is not None:
                deps[id(b.w)] = b.w
            for r in b.r:
                deps[id(r)] = r
        op = Op(eng, len(self.ops[eng]), fn, list(deps.values()), dma)
        if dma is not None:
            dma.semval += 16
            op.semval = dma.semval
        for b in reads:
            b.r.append(op)
        for b in writes:
            b.w = op
            b.r = []
        self.ops[eng].append(op)
        return op

    def emit(self, nc, es):
        def plan(eng_name, mark):
            waited = {}
            out = []
            for op in self.ops[eng_name]:
                need = {}
                for d in op.deps:
                    if d.dma is not None:
                        key = ("d", id(d.dma))
                        val = d.semval
                    else:
                        if d.eng == "pe" and eng_name == "pe" and op.dma is None:
                            continue
                        key = ("e", d.eng)
                        val = d.idx
                    if waited.get(key, -1) >= val:
                        continue
                    if key not in need or need[key][0] < val:
                        need[key] = (val, d)
                for key, (val, d) in need.items():
                    waited[key] = val
                    if mark and d.dma is None:
                        d.signal = True
                out.append(list(need.values()))
            return out

        for e in ENGS:
            plan(e, True)
        for e in ENGS:
            cnt = 0
            for op in self.ops[e]:
                if op.dma is None and op.signal:
                    cnt += 1
                    op.sigcount = cnt
        plans = {e: plan(e, False) for e in ENGS}
        sems = {e: es.enter_context(nc.semaphore("sem_" + e)) for e in ENGS}
        dma_bufs = {}
        for e in ENGS:
            for op in self.ops[e]:
                if op.dma is not None and id(op.dma) not in dma_bufs:
                    dma_bufs[id(op.dma)] = op.dma
        for b in dma_bufs.values():
            b.sem = es.enter_context(nc.semaphore("dsem_" + b.name))
        block = es.enter_context(nc.Block())

        def run_engine(eng_name, handle):
            for op, needs in zip(self.ops[eng_name], plans[eng_name]):
                for val, d in needs:
                    if d.dma is not None:
                        handle.wait_ge(d.dma.sem, d.semval)
                    else:
                        handle.wait_ge(sems[d.eng], d.sigcount)
                ins = op.fn(handle)
                if op.dma is not None:
                    ins.then_inc(op.dma.sem, 16)
                elif op.signal:
                    ins.then_inc(sems[eng_name], 1)

        @block.tensor
        def _(h):
            run_engine("pe", h)

        @block.scalar
        def _(h):
            run_engine("act", h)

        @block.vector
        def _(h):
            run_engine("dve", h)

        @block.gpsimd
        def _(h):
            run_engine("pool", h)

        @block.sync
        def _(h):
            run_engine("sp", h)


def build(T, dbg=False, stop=99):
    NST = T // STK
    NQB = T // 128
    nc = bass.Bass("TRN2", target_bir_lowering=False)
    x_d = nc.dram_tensor("x", [T, D], F32, kind="ExternalInput").ap()
    ws_d = nc.dram_tensor("ws", [NCH, 128, 2048], F32, kind="ExternalInput").ap()
    cf_d = nc.dram_tensor("cf", [128, NCF], F32, kind="ExternalInput").ap()
    lnp_d = nc.dram_tensor("lnp", [6, 128, D], F32, kind="ExternalInput").ap()
    out_d = nc.dram_tensor("out", [T, D], F32, kind="ExternalOutput").ap()
    if dbg:
        dbg1_d = nc.dram_tensor("dbg1", [T, D], F32, kind="ExternalOutput").ap()
        dbg2_d = nc.dram_tensor("dbg2", [T, D], F32, kind="ExternalOutput").ap()
        dbg3_d = nc.dram_tensor("dbg3", [T, D], F32, kind="ExternalOutput").ap()

    es = ExitStack()
    S = Sched()

    def sb(name, shape, dt):
        return es.enter_context(nc.sbuf_tensor("sb_" + name, shape, dt))

    Kc = sb("Kc", [128, 4, T], BF16)
    Vc = sb("Vc", [128, NQB, 520], BF16)
    Kix = sb("Kix", [128, T], BF16)
    ring = sb("ring", [128, RING, 2048], BF16)
    res = sb("res", [128, 4, D], F32)
    xT = sb("xT", [128, 8, STK], BF16)
    bigb = sb("bigb", [128, 16384], BF16)
    acc = sb("acc", [128, 4096], F32)
    rh = sb("rh", [128, 2, 512], F32)
    PT = sb("PT", [128, 4, 512], BF16)
    gb = sb("gb", [128, 2, D], F32)
    omix = sb("omix", [128, D], F32)
    omixg = sb("omixg", [128, 2, 512], F32)
    cf = sb("cf", [128, NCF], F32)
    identb = sb("identb", [128, 128], BF16)
    ident4 = sb("ident4", [128, 512], BF16)
    qgbd = sb("qgbd", [128, 2, 4, 256], BF16)
    kgT = sb("kgT", [128, 2, STK], BF16)
    ktok = sb("ktok", [128, 4, 256], BF16)
    vg = sb("vg", [128, 4, 512], BF16)
    gsil = sb("gsil", [128, 4, 512], BF16)
    ATt = sb("ATt", [128, 512], BF16)
    Sst = sb("Sst", [128, 2, 128], F32)
    Sbt = sb("Sbt", [128, 2, 128], BF16)
    wabs = sb("wabs", [128, 4, 8], F32)
    wsgn = sb("wsgn", [128, 4, 8], F32)
    small = sb("small", [128, 256], F32)
    tauc = sb("tauc", [128, 1], F32)
    seg8 = sb("seg8", [128, 32, 8], F32)

    aT = bigb[:, 0:NFC * 512].rearrange("p (f t) -> p f t", f=NFC)
    mbs = [bigb[:, 0:4096], bigb[:, 12288:16384]]
    qz = bigb[:, 4096:8192].rearrange("p (h t) -> p h t", h=8)
    qbd = bigb[:, 8192:12288].rearrange("p (a b c) -> p a b c", a=4, b=4)
    E1 = acc[:, 0:1024].rearrange("p (a t) -> p a t", a=2)
    E2 = acc[:, 1024:2048].rearrange("p (a t) -> p a t", a=2)
    E2tok = acc[:, 2048:3072].rearrange("p (a t) -> p a t", a=4)
    gaT = acc[:, 3072:3584]
    spt = acc[:, 3584:3840]
    gbv = gb[:, 0, :].bitcast(BF16)
    Dg = gbv[:, 0:1024].rearrange("p (h q) -> p h q", h=8)
    rhb = gb[:, 1, :].bitcast(BF16).rearrange("p (a t) -> p a t", a=4)
    identf = cf[:, CF_ID:CF_ID + 128]
    trin = cf[:, CF_TRIN:CF_TRIN + 128]
    trip = cf[:, CF_TRIP:CF_TRIP + 128]
    cmask = cf[:, CF_CM:CF_CM + 128]
    wa2p = cf[:, CF_WA2:CF_WA2 + 256]
    ba_b = cf[:, CF_BA:CF_BA + 256]
    gn = cf[:, CF_GN:CF_GN + 128]
    pow2 = cf[:, CF_P2:CF_P2 + KIT + 1]

    def sm(a, b):
        return small[:, a:b]

    ps = [es.enter_context(nc.psum_tensor("ps%d" % i, [128, 512], F32)) for i in range(8)]
    PB = [Buf("ps%d" % i, excl=True) for i in range(8)]

    B = {}

    def bf(name):
        if name not in B:
            B[name] = Buf(name)
        return B[name]

    cfB = bf("cf")
    resB = [bf("res%d" % i) for i in range(4)]
    xTB = [bf("xT%d" % i) for i in range(8)]
    ringB = [bf("ring%d" % i) for i in range(RING)]
    bigB = bf("bigb")
    aTB = [bf("aT%d" % i) for i in range(NFC)]
    accB = bf("acc")
    rhB = [bf("rh0"), bf("rh1")]
    PTB = [bf("PT%d" % i) for i in range(4)]
    gbB = bf("gb")
    gbP = bf("gbP")
    rhbB = [bf("rhb%d" % i) for i in range(4)]
    omixB = bf("omix")
    omixgB = [bf("omixg0"), bf("omixg1")]
    KcB = [bf("Kc%d" % i) for i in range(NST)]
    VcB = [bf("Vc%d" % i) for i in range(NST)]
    KixB = [bf("Kix%d" % i) for i in range(NST)]

    bank_rr = [0]

    def next_bank(pool=(0, 1, 2, 3, 4, 5, 6, 7)):
        bank_rr[0] += 1
        return pool[bank_rr[0] % len(pool)]

    def mm(out, lhsT, rhs, start, stop, reads, writes, skip=False):
        S.add("pe", lambda e: e.matmul(out, lhsT, rhs, start=start, stop=stop, skip_group_check=skip), reads, writes)

    def tr(out, in_, reads, writes):
        S.add("pe", lambda e: e.transpose(out, in_, identf), list(reads) + [cfB], writes)

    def act(out, in_, func, reads, writes, scale=1.0, bias=0.0):
        S.add("act", lambda e: e.activation(out, in_, func, bias=bias, scale=scale), reads, writes)

    def cp(eng, out, in_, reads, writes):
        if eng == "act":
            S.add("act", lambda e: e.activation(out, in_, AF.Copy), reads, writes)
        else:
            S.add(eng, lambda e: e.tensor_copy(out, in_), reads, writes)

    def tt(eng, out, in0, in1, op, reads, writes):
        S.add(eng, lambda e: e.tensor_tensor(out, in0, in1, op), reads, writes)

    def ts(eng, out, in0, s1, s2, op0, op1, reads, writes, accum=None):
        if op1 is None:
            S.add(eng, lambda e: e.tensor_scalar(out, in0, s1, None, op0), reads, writes)
        else:
            S.add(eng, lambda e: e.tensor_scalar(out, in0, s1, s2, op0, op1, accum_out=accum), reads, writes)

    def stt(out, in0, scalar, in1, op0, op1, reads, writes):
        S.add("dve", lambda e: e.scalar_tensor_tensor(out, in0, scalar, in1, op0, op1), reads, writes)

    evac_rr = [0]

    def evac_eng():
        evac_rr[0] += 1
        return "act" if evac_rr[0] % 2 else "dve"

    wstate = {"issued": 0, "cur": 0}
    total_chunks = NST * NCH

    def w_issue():
        j = wstate["issued"]
        if j >= total_chunks:
            return
        slot = j % RING
        src = ws_d[j % NCH]
        S.add("pool", lambda e: e.dma_start(out=ring[:, slot, :], in_=src, max_dma_last_dim=4096),
              reads=[], writes=[ringB[slot]], dma=ringB[slot])
        wstate["issued"] = j + 1

    def w_next():
        j = wstate["cur"]
        while wstate["issued"] < min(j + RING - 1, total_chunks) or wstate["issued"] <= j:
            w_issue()
        wstate["cur"] = j + 1
        slot = j % RING
        return ring[:, slot, :], ringB[slot]

    S.add("sp", lambda e: e.dma_start(out=cf[:], in_=cf_d), reads=[], writes=[cfB], dma=cfB)
    identB_ = bf("identb")
    cp("dve", identb[:], identf, [cfB], [identB_])
    for i in range(4):
        cp("dve", ident4[:, i * 128:(i + 1) * 128], identf, [cfB], [identB_])
    S.add("dve", lambda e: e.memset(bigb[:, 4096:12288], 0.0), [], [bigB])
    S.add("dve", lambda e: e.memset(Vc[:], 1.0), [], VcB)
    S.add("dve", lambda e: e.memset(Sst[:], 0.0), [], [bf("S")])
    S.add("dve", lambda e: e.memset(Sbt[:], 0.0), [], [bf("Sb")])
    S.add("dve", lambda e: e.memset(tauc[:], -1e29), [], [bf("tauc")])
    S.add("dve", lambda e: e.memset(qgbd[:], 0.0), [], [bf("qgbd")])

    def tile_to_xT(src, src_bufs, t4):
        tsl = slice(t4 * 128, (t4 + 1) * 128)
        for half in range(2):
            bank = next_bank((0, 1, 2, 3))
            for i in range(4):
                c = half * 4 + i
                tr(ps[bank][:, i * 128:(i + 1) * 128], src[:, c * 128:(c + 1) * 128], src_bufs, [PB[bank]])
            cp("act", xT[:, half * 4:half * 4 + 4, tsl], ps[bank][:].rearrange("p (a t) -> p a t", a=4),
               [PB[bank]], [xTB[half * 4 + i] for i in range(4)])

    def to_xT(_unused):
        for t4 in range(4):
            tile_to_xT(res[:, t4, :], [resB[t4]], t4)

    stg = [omix[:, :], rh[:, :, :].rearrange("p a t -> p (a t)")]
    stgB = [[omixB], [rhB[0], rhB[1]]]

    def prefetch_x_tile(s_next, t4):
        k = t4 % 2
        src = x_d[s_next * STK + t4 * 128:s_next * STK + (t4 + 1) * 128, :]
        S.add("sp", lambda e: e.dma_start(out=stg[k], in_=src), reads=[], writes=stgB[k], dma=bf("stgd%d" % k))
        tile_to_xT(stg[k], stgB[k], t4)

    ln_hoisted = set()

    def ln_stage(i, t4):
        st6 = sm(t4 * 16, t4 * 16 + 12)
        mv = sm(t4 * 16 + 12, t4 * 16 + 14)
        lnv = sm(t4 * 16 + 14, t4 * 16 + 15)
        rstd = sm(t4 * 16 + 15, t4 * 16 + 16)
        nmr = sm(160 + t4, 161 + t4)
        sB = bf("lnstat%d" % t4)
        r = res[:, t4, :]
        if i == -1:
            S.add("dve", lambda e: e.bn_stats(st6[:, 0:6], res[:, t4, 0:512]), [resB[t4]], [sB])
            ln_hoisted.add(t4)
        elif i == 0:
            if t4 in ln_hoisted:
                ln_hoisted.discard(t4)
            else:
                S.add("dve", lambda e: e.bn_stats(st6[:, 0:6], res[:, t4, 0:512]), [resB[t4]], [sB])
            S.add("dve", lambda e: e.bn_stats(st6[:, 6:12], res[:, t4, 512:1024]), [resB[t4]], [sB])
            S.add("dve", lambda e: e.bn_aggr(mv, st6), [sB], [sB])
            ts("dve", lnv, mv[:, 1:2], EPS_LN, None, ALU.add, None, [sB], [sB])
        elif i == 1:
            act(lnv, lnv, AF.Ln, [sB], [sB])
            act(rstd, lnv, AF.Exp, [sB], [sB], scale=-0.5)
        elif i == 2:
            stt(nmr, mv[:, 0:1], -1.0, rstd, ALU.mult, ALU.mult, [sB], [sB])
        elif i == 3:
            act(r, r, AF.Identity, [sB, resB[t4]], [resB[t4]], scale=rstd, bias=nmr)
        else:
            tt("dve", r, r, gb[:, 0, :], ALU.mult, [resB[t4], gbB, gbP], [resB[t4]])
            tt("dve", r, r, gb[:, 1, :], ALU.add, [resB[t4], gbB, gbP], [resB[t4]])

    def layer_norm_all(k, first=0):
        for step in range(first, 5 + 3):
            for t4 in range(4):
                i = step - t4
                if first <= i <= 4:
                    ln_stage(i, t4)

    def load_gb(k):
        for j in range(2):
            src = lnp_d[2 * k + j]
            S.add("sp", (lambda e, j=j, src=src: e.dma_start(out=gb[:, j, :], in_=src)), reads=[], writes=[gbB, gbP], dma=gbB)

    def ffn(k, down_hook=None, gu_hook=None, do_ln=True):
        for fc in range(NFC):
            w, wB = w_next()
            wv = w.rearrange("p (c f) -> p c f", c=8)
            bg = (0, 2)[fc % 2]
            bu = (1, 3)[fc % 2]
            for c in range(8):
                mm(ps[bg][:], wv[:, c, 0:128], xT[:, c, :], c == 0, c == 7, [wB, xTB[c]], [PB[bg]])
            for c in range(8):
                mm(ps[bu][:], wv[:, c, 128:256], xT[:, c, :], c == 0, c == 7, [wB, xTB[c]], [PB[bu]])
            sl = fc % 2
            act(rh[:, sl, :], ps[bg][:], AF.Silu, [PB[bg]], [rhB[sl]])
            tt("dve", aT[:, fc, :], rh[:, sl, :], ps[bu][:], ALU.mult, [rhB[sl], PB[bu], bigB], [aTB[fc]])
            if gu_hook is not None:
                gu_hook(fc)
        for dh in range(2):
            for g in range(6):
                w, wB = w_next()
                wv = w.rearrange("p (i f) -> p i f", i=4)
                for i in range(4):
                    fc = 4 * g + i
                    if fc >= NFC:
                        break
                    for t4 in range(4):
                        mm(ps[4 + t4][:], aT[:, fc, t4 * 128:(t4 + 1) * 128], wv[:, i, :], fc == 0, fc == NFC - 1,
                           [wB, aTB[fc]], [PB[4 + t4]])
                if down_hook is not None:
                    down_hook(dh, g)
            for t4 in range(4):
                r = res[:, t4, dh * 512:(dh + 1) * 512]
                stt(r, ps[4 + t4][:], C_FFN, r, ALU.mult, ALU.add, [PB[4 + t4], resB[t4]], [resB[t4]])
                if dh == 1 and do_ln:
                    ln_stage(0, t4)
            if dh == 0:
                for t4 in range(4):
                    ln_stage(-1, t4)
        if do_ln:
            layer_norm_all(k, first=1)

    def barrier_bigb(reads_extra=()):
        d = sm(200, 201)
        S.add("dve", lambda e: e.memset(d, 0.0), [], list(aTB) + [bigB, gbP, bf("dummy")])

    def zero_qpads():
        S.add("dve", lambda e: e.memset(bigb[:, 4096:12288], 0.0), [bigB], [bf("qz"), bf("qbd")])

    def projection(s, after_iw=None):
        tok0 = s * STK
        qgB, kgB, qbdB, qzB = bf("qgbd"), bf("kgT"), bf("qbd"), bf("qz")
        def fm_chunk(names):
            w, wB = w_next()
            wv = w.rearrange("p (c f) -> p c f", c=8)
            for half in range(2):
                name = names[half]
                bank = next_bank()
                for c in range(8):
                    mm(ps[bank][:], wv[:, c, half * 128:(half + 1) * 128], xT[:, c, :], c == 0, c == 7,
                       [wB, xTB[c]], [PB[bank]])
                pb = PB[bank]
                pv = ps[bank]
                if name == "ga":
                    cp("act", gaT, pv[:], [pb], [accB])
                    for t4 in range(4):
                        b2 = next_bank()
                        mm(ps[b2][:, 0:256], gaT[:, t4 * 128:(t4 + 1) * 128], wa2p, True, True, [accB, cfB], [PB[b2]])
                        tt("dve", spt, ps[b2][:, 0:256], ba_b, ALU.add, [PB[b2], cfB], [accB])
                        act(spt, spt, AF.Exp, [accB], [accB], scale=-1.0)
                        ts("dve", spt, spt, 1.0, None, ALU.add, None, [accB], [accB])
                        act(spt, spt, AF.Ln, [accB], [accB])
                        b3 = next_bank()
                        for p in range(2):
                            mm(ps[b3][:, p * 128:(p + 1) * 128], spt[:, p * 128:(p + 1) * 128], trin, p == 0, p == 1,
                               [accB, cfB], [PB[b3]])
                        for p in range(2):
                            act(sm(120 + p * 4 + t4, 121 + p * 4 + t4), ps[b3][:, p * 128 + 127:p * 128 + 128], AF.Exp,
                                [PB[b3]], [bf("ebl")])
                            act(E1[:, p, t4 * 128:(t4 + 1) * 128], ps[b3][:, p * 128:(p + 1) * 128], AF.Exp,
                                [PB[b3]], [accB])
                            act(E2[:, p, t4 * 128:(t4 + 1) * 128], ps[b3][:, p * 128:(p + 1) * 128], AF.Exp,
                                [PB[b3]], [accB], scale=-1.0)
                        b4 = next_bank()
                        mm(ps[b4][:, 0:256], trip, spt, True, True, [accB, cfB], [PB[b4]])
                        act(E2tok[:, t4, :], ps[b4][:, 0:256], AF.Exp, [PB[b4]], [accB])
                elif name == "ik":
                    cp(evac_eng(), Kix[:, tok0:tok0 + STK], pv[:], [pb], [KixB[s]])
                elif name.startswith("gq"):
                    p = int(name[2])
                    for hh in range(2):
                        rows = slice(hh * 64, (hh + 1) * 64)
                        stt(qgbd[rows, p, :, hh * 128:(hh + 1) * 128],
                            pv[rows, :].rearrange("p (a t) -> p a t", a=4), 0.125,
                            E1[rows, p, :].rearrange("p (a t) -> p a t", a=4), ALU.mult, ALU.mult,
                            [pb, accB], [qgB])
                elif name.startswith("gk"):
                    p = int(name[2])
                    tt("dve", kgT[:, p, :], pv[:], E2[:, p, :], ALU.mult, [pb, accB], [kgB])
                elif name.startswith("dq"):
                    p = int(name[2])
                    for hh in range(2):
                        rows = slice(hh * 64, (hh + 1) * 64)
                        cp("act", qbd[rows, p, :, hh * 128:(hh + 1) * 128],
                           pv[rows, :].rearrange("p (a t) -> p a t", a=4), [pb, bigB], [qbdB])
                elif name.startswith("dk"):
                    p = int(name[2])
                    cp("act", Kc[:, p, tok0:tok0 + STK], pv[:], [pb], [KcB[s]])
                elif name.startswith("iq"):
                    p = int(name[2])
                    for hh in range(2):
                        rows = slice(hh * 64, (hh + 1) * 64)
                        cp(evac_eng(), qz[rows, 2 * p + hh, :], pv[rows, :], [pb, bigB], [qzB])
        ktB, vgB, gsB, wB_ = bf("ktok"), bf("vg"), bf("gsil"), bf("wabs")

        def tm_chunk(name):
            w, wB = w_next()
            wv = w.rearrange("p (c f) -> p c f", c=8)
            ncol = 8 if name == "iw" else 256
            for t4 in range(4):
                bank = next_bank()
                for c in range(8):
                    mm(ps[bank][:, 0:ncol], xT[:, c, t4 * 128:(t4 + 1) * 128], wv[:, c, 0:ncol], c == 0, c == 7,
                       [wB, xTB[c]], [PB[bank]])
                pb = PB[bank]
                pv = ps[bank][:, 0:ncol]
                if name == "gk":
                    tt("dve", ktok[:, t4, :], pv, E2tok[:, t4, :], ALU.mult, [pb, accB], [ktB])
                elif name.startswith("gv"):
                    i = int(name[2])
                    cp("act", vg[:, t4, i * 256:(i + 1) * 256], pv, [pb], [vgB])
                elif name.startswith("gg"):
                    i = int(name[2])
                    sl = t4 % 2
                    act(rh[:, sl, 0:256], pv, AF.Silu, [pb], [rhB[sl]])
                    tt("dve", gsil[:, t4, i * 256:(i + 1) * 256].rearrange("p (h v) -> p h v", h=2),
                       rh[:, sl, 0:256].rearrange("p (h v) -> p h v", h=2),
                       gn.unsqueeze(1).to_broadcast([128, 2, 128]), ALU.mult, [rhB[sl], cfB], [gsB])
                elif name.startswith("dv"):
                    i = int(name[2])
                    dst = Vc[:, s * 4 + t4, :].rearrange("p (h e) -> p h e", e=65)[:, 4 * i:4 * i + 4, 0:64]
                    cp("act", dst, pv.rearrange("p (h e) -> p h e", e=64), [pb], [VcB[s]])
                elif name == "iw":
                    ts("dve", wsgn[:, t4, :], pv, 0.0, 2.0, ALU.is_ge, ALU.mult, [pb], [wB_])
                    ts("dve", wsgn[:, t4, :], wsgn[:, t4, :], -1.0, None, ALU.add, None, [wB_], [wB_])
                    stt(wabs[:, t4, :], pv, C_IDX, wsgn[:, t4, :], ALU.mult, ALU.mult, [pb, wB_], [wB_])

        for names in (("ga", "ik"), ("gq0", "gq1"), ("gk0", "gk1"), ("iq0", "iq1"), ("iq2", "iq3")):
            fm_chunk(names)
        tm_chunk("gk")
        tm_chunk("gg0")
        tm_chunk("gg1")
        tm_chunk("iw")
        if after_iw is not None:
            after_iw()
        for names in (("dq0", "dq1"), ("dq2", "dq3"), ("dk0", "dk1"), ("dk2", "dk3")):
            fm_chunk(names)
        for name in ("gv0", "gv1", "dv0", "dv1"):
            tm_chunk(name)

    def gla(s, t4):
        qgB, kgB, ktB, vgB, gsB = bf("qgbd"), bf("kgT"), bf("ktok"), bf("vg"), bf("gsil")
        SB_, SbB, ATB = bf("S"), bf("Sb"), bf("AT")
        tsl = slice(t4 * 128, (t4 + 1) * 128)
        bA = next_bank()
        for p in range(2):
            mm(ps[bA][:, p * 256:(p + 1) * 256], kgT[:, p, tsl], qgbd[:, p, t4, :], p == 0, p == 1,
               [kgB, qgB], [PB[bA]])
        tt("dve", ATt[:].rearrange("p (h t) -> p h t", h=4), ps[bA][:].rearrange("p (h t) -> p h t", h=4),
           cmask.unsqueeze(1).to_broadcast([128, 4, 128]), ALU.mult, [PB[bA], cfB], [ATB])
        bo = next_bank()
        for h in range(4):
            p = h // 2
            mm(ps[bo][:, h * 128:(h + 1) * 128], ATt[:, h * 128:(h + 1) * 128], vg[:, t4, h * 128:(h + 1) * 128],
               h == 0, False, [ATB, vgB], [PB[bo]])
            mm(ps[bo][:, h * 128:(h + 1) * 128], qgbd[:, p, t4, (h % 2) * 128:(h % 2 + 1) * 128], Sbt[:, p, :],
               False, h == 3, [qgB, SbB], [PB[bo]])
        bS = next_bank()
        for p in range(2):
            mm(ps[bS][:, p * 256:(p + 1) * 256], ktok[:, t4, p * 128:(p + 1) * 128], vg[:, t4, p * 256:(p + 1) * 256],
               p == 0, p == 1, [ktB, vgB], [PB[bS]])
        for p in range(2):
            for hh in range(2):
                rows = slice(hh * 64, (hh + 1) * 64)
                tt("dve", Sst[rows, p, :], Sst[rows, p, :], ps[bS][rows, p * 256 + hh * 128:p * 256 + (hh + 1) * 128],
                   ALU.add, [SB_, PB[bS]], [SB_])
                ts("dve", Sst[rows, p, :], Sst[rows, p, :], small[rows, 120 + p * 4 + t4:121 + p * 4 + t4], None,
                   ALU.mult, None, [SB_, bf("ebl")], [SB_])
            cp("act", Sbt[:, p, :], Sst[:, p, :], [SB_], [SbB])
        ot = rh[:, 0, :]
        sq = rh[:, 1, :]
        ssq = sm(64, 68)
        rs = sm(68, 72)
        gB = bf("glastat")
        cp("act", ot, ps[bo][:], [PB[bo]], [rhB[0]])
        tt("dve", sq, ot, ot, ALU.mult, [rhB[0]], [rhB[1]])
        S.add("dve", lambda e: e.tensor_reduce(ssq, sq.rearrange("p (h v) -> p h v", h=4), AX.X, ALU.add),
              [rhB[1]], [gB])
        ts("dve", ssq, ssq, 1.0 / 128.0, RMS_EPS, ALU.mult, ALU.add, [gB], [gB])
        act(ssq, ssq, AF.Ln, [gB], [gB])
        act(rs, ssq, AF.Exp, [gB], [gB], scale=-0.5)
        tt("dve", ot.rearrange("p (h v) -> p h v", h=4), ot.rearrange("p (h v) -> p h v", h=4),
           rs.unsqueeze(2).to_broadcast([128, 4, 128]), ALU.mult, [rhB[0], gB], [rhB[0]])
        tt("dve", omixg[:, t4 % 2, :], ot, gsil[:, t4, :], ALU.mult, [rhB[0], gsB], [omixgB[t4 % 2]])

    def dsa_select(s, t4):
        qb = s * 4 + t4
        N = (qb + 1) * 128
        nkt = (N + 511) // 512
        tsl = slice(t4 * 128, (t4 + 1) * 128)
        mb = mbs[qb % 2]
        qzB, wB_, mbB, tauB = bf("qz"), bf("wabs"), bf("mb%d" % (qb % 2)), bf("tau")
        kixr = [KixB[i] for i in range(s + 1)]
        DB = bf("Dg")
        for h in range(8):
            ts("dve", Dg[:, h, :], identb[:], wsgn[:, t4, h:h + 1], None, ALU.mult, None,
               [bf("identb"), wB_, gbP], [DB])
        LAG = 3
        steps = [(kt, h) for kt in range(nkt) for h in range(8)]
        info = {}

        def score(i):
            kt, h = steps[i]
            ncol = min(512, N - kt * 512)
            bank = next_bank((2, 3, 4, 5, 6, 7))
            sl = i % 4
            mm(ps[bank][:, 0:ncol], qz[:, h, tsl], Kix[:, kt * 512:kt * 512 + ncol], True, True, [qzB] + kixr,
               [PB[bank]])
            if i % 2 == 0:
                act(rhb[:, sl, 0:ncol], ps[bank][:, 0:ncol], AF.Relu, [PB[bank], wB_, gbP], [rhbB[sl]],
                    scale=wabs[:, t4, h:h + 1])
            else:
                ts("dve", rhb[:, sl, 0:ncol], ps[bank][:, 0:ncol], wabs[:, t4, h:h + 1], 0.0, ALU.mult, ALU.max,
                   [PB[bank], wB_, gbP], [rhbB[sl]])

        def hsum(i):
            kt, h = steps[i]
            ncol = min(512, N - kt * 512)
            bsum = (0, 1)[kt % 2]
            sl = i % 4
            mm(ps[bsum][:, 0:ncol], Dg[:, h, :], rhb[:, sl, 0:ncol], h == 0, h == 7, [DB, rhbB[sl], gbP], [PB[bsum]])
            if h == 7:
                cp("act", acc[:, kt * 512:kt * 512 + ncol], ps[bsum][:, 0:ncol], [PB[bsum]], [accB])

        for i in range(len(steps) + LAG):
            if i < len(steps):
                score(i)
            if i >= LAG:
                hsum(i - LAG)
        a = acc[:, 0:N]
        if qb >= 2:
            mn = sm(80, 81)
            mx8 = sm(88, 96)
            cc = sm(81, 82)
            h0 = sm(82, 83)
            cnt = sm(83, 84)
            pm = sm(84, 85)
            hk = sm(96, 96 + KIT + 1)
            tau = sm(85, 86)
            S.add("dve", lambda e: e.memset(acc[0:64, N - 64:N], -1e30), [], [accB])
            segv = a.rearrange("p (j g) -> p g j", g=32)
            for g in range(32):
                S.add("dve", lambda e, g=g: e.max(seg8[:, g, :], segv[:, g, :]), [accB], [bf("seg%d" % g)])
            segB = [bf("seg%d" % g) for g in range(32)]
            S.add("dve", lambda e: e.tensor_reduce(mn, seg8[:, :, 7], AX.X, ALU.min), segB, [tauB])
            S.add("dve", lambda e: e.tensor_reduce(mx8[:, 0:1], seg8[:, :, 7], AX.X, ALU.max), segB, [tauB])
            tt("dve", cc, mx8[:, 0:1], mn, ALU.add, [tauB], [tauB])
            ts("dve", cc, cc, 0.5, None, ALU.mult, None, [tauB], [tauB])
            tt("dve", h0, mx8[:, 0:1], mn, ALU.subtract, [tauB], [tauB])
            ts("dve", h0, h0, 0.5, None, ALU.mult, None, [tauB], [tauB])
            ts("dve", hk, pow2, h0, None, ALU.mult, None, [tauB, cfB], [tauB])
            for k in range(KIT):
                ts("dve", mb[:, 0:N], a, cc, 0.0, ALU.is_ge, ALU.add, [accB, tauB, bigB], [mbB, tauB], accum=cnt)
                ts("dve", pm, cnt, TOPK - 0.5, 0.5, ALU.is_ge, ALU.subtract, [tauB], [tauB])
                stt(cc, pm, hk[:, k:k + 1], cc, ALU.mult, ALU.add, [tauB], [tauB])
            tt("dve", tau, cc, hk[:, KIT:KIT + 1], ALU.subtract, [tauB], [tauB])
        else:
            S.add("dve", lambda e: e.memset(acc[0:64, N - 64:N], -1e30), [], [accB])
            tau = tauc[:, 0:1]
        ts("dve", mb[:, 0:N], a, tau, NEG, ALU.is_lt, ALU.mult, [accB, tauB, bf("tauc"), bigB], [mbB])

    def dsa_attend(s, t4):
        qb = s * 4 + t4
        mb = mbs[qb % 2]
        qbdB, mbB = bf("qbd"), bf("mb%d" % (qb % 2))
        kcr = [KcB[i] for i in range(s + 1)]
        vcr = [VcB[i] for i in range(s + 1)]
        pool_l = (2, 3, 4, 5, 6, 7)
        units = [(st, bk) for st in range(qb + 1) for bk in range(2)]
        LAG = 2
        ubank = {}

        def logits(u):
            st, bk = units[u]
            ssl = slice(st * 128, (st + 1) * 128)
            bank = next_bank(pool_l)
            ubank[u] = bank
            mm(ps[bank][:], mb[:, ssl], ident4[:], True, False, [mbB, bf("identb")], [PB[bank]])
            for pp in range(2):
                pair = 2 * bk + pp
                mm(ps[bank][:, pp * 256:(pp + 1) * 256], Kc[:, pair, ssl], qbd[:, pair, t4, :], False, pp == 1,
                   kcr + [qbdB], [PB[bank]])
            slot = u % 4
            act(PT[:, slot, :], ps[bank][:], AF.Exp, [PB[bank]], [PTB[slot]], scale=0.125)

        def pv(u):
            st, bk = units[u]
            slot = u % 4
            for hh in range(4):
                h = 4 * bk + hh
                mm(ps[bk][:, hh * 65:(hh + 1) * 65], PT[:, slot, hh * 128:(hh + 1) * 128],
                   Vc[:, st, h * 65:(h + 1) * 65], st == 0 and hh == 0, st == qb and hh == 3,
                   [PTB[slot]] + vcr, [PB[bk]])

        for u in range(len(units) + LAG):
            if u < len(units):
                logits(u)
            if u >= LAG:
                pv(u - LAG)
        for bk in range(2):
            rden = sm(72 + 4 * bk, 76 + 4 * bk)
            dB = bf("rden%d" % bk)
            pv = ps[bk][:, 0:260].rearrange("p (h e) -> p h e", e=65)
            S.add("dve", lambda e, rden=rden, pv=pv: e.reciprocal(rden.unsqueeze(2), pv[:, :, 64:65]), [PB[bk]], [dB])
            tt("dve", omix[:, 512 + bk * 256:512 + (bk + 1) * 256].rearrange("p (h e) -> p h e", e=64),
               pv[:, :, 0:64], rden.unsqueeze(2).to_broadcast([128, 4, 64]), ALU.mult, [PB[bk], dB], [omixB])

    def omix_to_xT(t4):
        tsl = slice(t4 * 128, (t4 + 1) * 128)
        for half in range(2):
            bank = next_bank((2, 3, 4, 5, 6, 7))
            for i in range(4):
                if half == 0:
                    tr(ps[bank][:, i * 128:(i + 1) * 128], omixg[:, t4 % 2, i * 128:(i + 1) * 128],
                       [omixgB[t4 % 2]], [PB[bank]])
                else:
                    tr(ps[bank][:, i * 128:(i + 1) * 128], omix[:, 512 + i * 128:512 + (i + 1) * 128],
                       [omixB], [PB[bank]])
            cp(evac_eng(), xT[:, half * 4:half * 4 + 4, tsl], ps[bank][:].rearrange("p (a t) -> p a t", a=4),
               [PB[bank]], [xTB[half * 4 + i] for i in range(4)])

    def wout():
        for j in range(4):
            w, wB = w_next()
            wv = w.rearrange("p (c f) -> p c f", c=8)
            for t4 in range(4):
                bank = next_bank()
                for c in range(8):
                    mm(ps[bank][:, 0:256], xT[:, c, t4 * 128:(t4 + 1) * 128], wv[:, c, :], c == 0, c == 7,
                       [wB, xTB[c]], [PB[bank]])
                r = res[:, t4, j * 256:(j + 1) * 256]
                stt(r, ps[bank][:, 0:256], C_MIX, r, ALU.mult, ALU.add, [PB[bank], resB[t4]], [resB[t4]])
                if j == 3:
                    ln_stage(0, t4)
            if j == 1:
                for t4 in range(4):
                    ln_stage(-1, t4)

    def dump(dst, s):
        S.add("sp", lambda e: e.dma_start(out=dst[s * STK:(s + 1) * STK, :].rearrange("(a p) d -> p a d", p=128),
                                          in_=res[:]), reads=resB, writes=[], dma=bf("dump"))

    deferred = [None]
    for s in range(NST):
        xin = x_d[s * STK:(s + 1) * STK, :].rearrange("(a p) d -> p a d", p=128)
        if s == 0:
            S.add("sp", lambda e, xin=xin: e.dma_start(out=res[:], in_=xin), reads=[], writes=resB, dma=bf("resld"))
            load_gb(0)
            to_xT(None)
        if stop >= 1:
            def gu_hook(fc):
                if fc >= 1 and deferred[0]:
                    deferred[0].pop(0)()
            ffn(0, gu_hook=gu_hook)
        if dbg:
            dump(dbg1_d, s)
        if stop >= 2:
            barrier_bigb()
            to_xT(None)
            zero_qpads()
            projection(s, after_iw=(lambda s=s: dsa_select(s, 0)))
            gla(s, 0)
        for t4 in range(4):
            if stop >= 3:
                if t4 < 3:
                    gla(s, t4 + 1)
                    dsa_select(s, t4 + 1)
                dsa_attend(s, t4)
                omix_to_xT(t4)
            if dbg and stop >= 3:
                S.add("sp", lambda e, s=s, t4=t4: e.dma_start(
                    out=dbg2_d[s * STK + t4 * 128:s * STK + (t4 + 1) * 128, :], in_=omix[:]),
                    reads=[omixB], writes=[], dma=bf("dump2"))
        if stop >= 6:
            load_gb(1)
            wout()
            layer_norm_all(1, first=1)
        if dbg:
            dump(dbg3_d, s)
        if stop >= 7:
            load_gb(2)
            to_xT(None)
            S.add("dve", lambda e: e.memset(sm(201, 202), 0.0), [], [bf("mb0"), bf("mb1"), bf("qz"), bf("qbd"), bigB, bf("dummy")])
            def hook(dh, g, s=s):
                if s + 1 < NST and dh == 0 and 1 <= g <= 4:
                    prefetch_x_tile(s + 1, g - 1)
            last = (s + 1 >= NST)
            ffn(2, down_hook=hook, do_ln=last)

        def store_out(s=s):
            S.add("sp", lambda e: e.dma_start(
                out=out_d[s * STK:(s + 1) * STK, :].rearrange("(a p) d -> p a d", p=128), in_=res[:]),
                reads=resB, writes=[], dma=bf("outst"))

        if s + 1 >= NST or stop < 7:
            store_out()
        else:
            def make_tail(s=s, store_out=store_out):
                steps = []
                for i in range(5):
                    steps.append(lambda i=i: [ln_stage(i, t4) for t4 in range(4)])

                def fin():
                    store_out()
                    xn = x_d[(s + 1) * STK:(s + 2) * STK, :].rearrange("(a p) d -> p a d", p=128)
                    S.add("sp", lambda e: e.dma_start(out=res[:], in_=xn), reads=[], writes=resB, dma=bf("resld"))
                    load_gb(0)
                steps.append(fin)
                return steps
            deferred[0] = make_tail()
    fin = []
    for name in (["outst"] + (["dump", "dump2"] if dbg else [])):
        b = bf(name)
        fin.append(b)
    S.add("sp", lambda e: e.nop(), reads=[], writes=fin + resB + [omixB] + omixgB)
    S.emit(nc, es)
    es.close()
    return nc


def _chunk_cols(w, cols_list):
    out = np.zeros((128, 8, 256), np.float32)
    o = 0
    for cols in cols_list:
        sel = w[:, cols]
        n = sel.shape[1]
        out[:, :, o:o + n] = sel.reshape(8, 128, n).transpose(1, 0, 2)
        o += n
    return out.reshape(128, 2048)


def _build_ws(w_in, w_out, w_gu1, w_d1, w_gu2, w_d2):
    chunks = []

    def ffn_chunks(w_gu, w_d):
        for fc in range(NFC):
            chunks.append(_chunk_cols(w_gu, [np.arange(fc * 128, (fc + 1) * 128),
                                             np.arange(DFF + fc * 128, DFF + (fc + 1) * 128)]))
        for dh in range(2):
            for g in range(6):
                a = np.zeros((128, 4, 512), np.float32)
                for i in range(4):
                    fc = 4 * g + i
                    if fc < NFC:
                        a[:, i, :] = w_d[fc * 128:(fc + 1) * 128, dh * 512:(dh + 1) * 512]
                chunks.append(a.reshape(128, 2048))

    ffn_chunks(w_gu1, w_d1)
    r = np.arange
    ik = np.concatenate([r(3600, 3664), r(3600, 3664)])
    fmA = [r(1536, 1664), ik, r(0, 128), r(128, 256), r(256, 384), r(384, 512)]
    fmA += [r(3088 + p * 128, 3088 + (p + 1) * 128) for p in range(4)]
    for i in range(5):
        chunks.append(_chunk_cols(w_in, [fmA[2 * i], fmA[2 * i + 1]]))
    for cols in (r(256, 512), r(1024, 1280), r(1280, 1536), r(3664, 3672)):
        chunks.append(_chunk_cols(w_in, [cols]))
    fmB = [r(1552 + p * 128, 1552 + (p + 1) * 128) for p in range(4)]
    fmB += [r(2064 + p * 128, 2064 + (p + 1) * 128) for p in range(4)]
    for i in range(4):
        chunks.append(_chunk_cols(w_in, [fmB[2 * i], fmB[2 * i + 1]]))
    for cols in (r(512, 768), r(768, 1024), r(2576, 2832), r(2832, 3088)):
        chunks.append(_chunk_cols(w_in, [cols]))
    for j in range(4):
        chunks.append(_chunk_cols(w_out, [r(j * 256, (j + 1) * 256)]))
    ffn_chunks(w_gu2, w_d2)
    assert len(chunks) == NCH
    return np.ascontiguousarray(np.stack(chunks, 0))


def _build_cf(w_gla_a2, b_gla_a, gla_norm_g):
    cf = np.zeros((128, NCF), np.float32)
    cf[:, CF_ID:CF_ID + 128] = np.eye(128, dtype=np.float32)
    tp = np.arange(128)[:, None]
    t = np.arange(128)[None, :]
    le = (tp <= t).astype(np.float32)
    cf[:, CF_TRIN:CF_TRIN + 128] = le * np.float32(-1.0 / 16.0)
    cf[:, CF_TRIP:CF_TRIP + 128] = le * np.float32(1.0 / 16.0)
    cf[:, CF_CM:CF_CM + 128] = le
    cf[0:16, CF_WA2:CF_WA2 + 256] = w_gla_a2
    cf[:, CF_BA:CF_BA + 256] = b_gla_a[None, :]
    cf[:, CF_GN:CF_GN + 128] = gla_norm_g[None, :]
    cf[:, CF_P2:CF_P2 + KIT + 1] = (2.0 ** -np.arange(KIT + 1, dtype=np.float64)).astype(np.float32)[None, :]
    return cf


_NC_CACHE = {}


def _run(inputs, T, dbg=False, trace=False, stop=99):
    x = np.asarray(inputs["x"], np.float32)
    ws = _build_ws(np.asarray(inputs["w_in"][0], np.float32), np.asarray(inputs["w_out"][0], np.float32),
                   np.asarray(inputs["ffn1_w_gu"][0], np.float32), np.asarray(inputs["ffn1_w_down"][0], np.float32),
                   np.asarray(inputs["ffn2_w_gu"][0], np.float32), np.asarray(inputs["ffn2_w_down"][0], np.float32))
    cf = _build_cf(np.asarray(inputs["w_gla_a2"][0], np.float32), np.asarray(inputs["b_gla_a"][0], np.float32),
                   np.asarray(inputs["gla_norm_g"][0], np.float32))
    lnp = np.stack([np.asarray(inputs[k][0], np.float32) for k in
                    ("ln1_g", "ln1_b", "ln2_g", "ln2_b", "ln3_g", "ln3_b")], 0)
    lnp = np.ascontiguousarray(np.broadcast_to(lnp[:, None, :], (6, 128, D)))
    key = (T, dbg, stop)
    if key not in _NC_CACHE:
        _NC_CACHE[key] = build(T, dbg, stop)
    nc = _NC_CACHE[key]
    nb = x.shape[0]
    in_maps = [{"x": np.ascontiguousarray(x[b, :T]), "ws": ws, "cf": cf, "lnp": lnp} for b in range(nb)]
    res = run_bass_kernel_spmd(nc, in_maps, core_ids=list(range(nb)), trace=trace)
    return res


def kernel(**inputs):
    res = _run(inputs, SEQ)
    out = np.stack([np.asarray(r["out"], np.float32) for r in res.results], 0)
    return out
```

```python
from contextlib import ExitStack

import numpy as np
import concourse.bass as bass
import concourse.mybir as mybir
from concourse.alu_op_type import AluOpType as ALU
from concourse.bass_utils import run_bass_kernel_spmd

F32 = mybir.dt.float32
BF16 = mybir.dt.bfloat16
AF = mybir.ActivationFunctionType
AX = mybir.AxisListType

D = 1024
DFF = 2816
NFC = 22
SEQ = 4096
NCORES = 8
STK = 512
ALPHA = 2.0 ** 0.25
C_FFN = 0.5 / ALPHA
C_MIX = 1.0 / ALPHA
EPS_LN = 1e-5 / (ALPHA * ALPHA)
RMS_EPS = 1e-6
C_IDX = (64.0 ** -0.5) * (8.0 ** -0.5)
NEG = -30000.0
KIT = 12
TOPK = 256
RING = 3
NCH = 22 + 12 + 9 + 8 + 4 + 22 + 12

CF_ID = 0
CF_TRIN = 128
CF_TRIP = 256
CF_CM = 384
CF_WA2 = 512
CF_BA = 768
CF_GN = 1024
CF_P2 = 1152
NCF = 1152 + 32


class Buf:
    __slots__ = ("name", "w", "r", "sem", "semval", "excl")

    def __init__(self, name, excl=False):
        self.name = name
        self.excl = excl
        self.w = None
        self.r = []
        self.sem = None
        self.semval = 0


class Op:
    __slots__ = ("eng", "idx", "fn", "deps", "dma", "signal", "sigcount", "semval")

    def __init__(self, eng, idx, fn, deps, dma):
        self.eng = eng
        self.idx = idx
        self.fn = fn
        self.deps = deps
        self.dma = dma
        self.signal = False
        self.sigcount = 0
        self.semval = 0


ENGS = ("pe", "act", "dve", "pool", "sp")


class Sched:
    def __init__(self):
        self.ops = {e: [] for e in ENGS}

    def add(self, eng, fn, reads=(), writes=(), dma=None):
        deps = {}
        for b in reads:
            if b.w is not None:
                deps[id(b.w)] = b.w
            if b.excl:
                for r in b.r:
                    if r.eng != eng:
                        deps[id(r)] = r
        for b in writes:
            if b.w is not None:
                deps[id(b.w)] = b.w
            for r in b.r:
                deps[id(r)] = r
        op = Op(eng, len(self.ops[eng]), fn, list(deps.values()), dma)
        if dma is not None:
            dma.semval += 16
            op.semval = dma.semval
        for b in reads:
            b.r.append(op)
        for b in writes:
            b.w = op
            b.r = []
        self.ops[eng].append(op)
        return op

    def emit(self, nc, es):
        def plan(eng_name, mark):
            waited = {}
            out = []
            for op in self.ops[eng_name]:
                need = {}
                for d in op.deps:
                    if d.dma is not None:
                        key = ("d", id(d.dma))
                        val = d.semval
                    else:
                        if d.eng == "pe" and eng_name == "pe" and op.dma is None:
                            continue
                        key = ("e", d.eng)
                        val = d.idx
                    if waited.get(key, -1) >= val:
                        continue
                    if key not in need or need[key][0] < val:
                        need[key] = (val, d)
                for key, (val, d) in need.items():
                    waited[key] = val
                    if mark and d.dma is None:
                        d.signal = True
                out.append(list(need.values()))
            return out

        for e in ENGS:
            plan(e, True)
        for e in ENGS:
            cnt = 0
            for op in self.ops[e]:
                if op.dma is None and op.signal:
                    cnt += 1
                    op.sigcount = cnt
        plans = {e: plan(e, False) for e in ENGS}
        sems = {e: es.enter_context(nc.semaphore("sem_" + e)) for e in ENGS}
        dma_bufs = {}
        for e in ENGS:
            for op in self.ops[e]:
                if op.dma is not None and id(op.dma) not in dma_bufs:
                    dma_bufs[id(op.dma)] = op.dma
        for b in dma_bufs.values():
            b.sem = es.enter_context(nc.semaphore("dsem_" + b.name))
        block = es.enter_context(nc.Block())

        def run_engine(eng_name, handle):
            for op, needs in zip(self.ops[eng_name], plans[eng_name]):
                for val, d in needs:
                    if d.dma is not None:
                        handle.wait_ge(d.dma.sem, d.semval)
                    else:
                        handle.wait_ge(sems[d.eng], d.sigcount)
                ins = op.fn(handle)
                if op.dma is not None:
                    ins.then_inc(op.dma.sem, 16)
                elif op.signal:
                    ins.then_inc(sems[eng_name], 1)

        @block.tensor
        def _(h):
            run_engine("pe", h)

        @block.scalar
        def _(h):
            run_engine("act", h)

        @block.vector
        def _(h):
            run_engine("dve", h)

        @block.gpsimd
        def _(h):
            run_engine("pool", h)

        @block.sync
        def _(h):
            run_engine("sp", h)


def build(T, dbg=False, stop=99):
    NST = T // STK
    NQB = T // 128
    nc = bass.Bass("TRN2", target_bir_lowering=False)
    x_d = nc.dram_tensor("x", [T, D], F32, kind="ExternalInput").ap()
    ws_d = nc.dram_tensor("ws", [NCH, 128, 2048], F32, kind="ExternalInput").ap()
    cf_d = nc.dram_tensor("cf", [128, NCF], F32, kind="ExternalInput").ap()
    lnp_d = nc.dram_tensor("lnp", [6, 128, D], F32, kind="ExternalInput").ap()
    out_d = nc.dram_tensor("out", [T, D], F32, kind="ExternalOutput").ap()
    if dbg:
        dbg1_d = nc.dram_tensor("dbg1", [T, D], F32, kind="ExternalOutput").ap()
        dbg2_d = nc.dram_tensor("dbg2", [T, D], F32, kind="ExternalOutput").ap()
        dbg3_d = nc.dram_tensor("dbg3", [T, D], F32, kind="ExternalOutput").ap()

    es = ExitStack()
    S = Sched()

    def sb(name, shape, dt):
        return es.enter_context(nc.sbuf_tensor("sb_" + name, shape, dt))

    Kc = sb("Kc", [128, 4, T], BF16)
    Vc = sb("Vc", [128, NQB, 520], BF16)
    Kix = sb("Kix", [128, T], BF16)
    ring = sb("ring", [128, RING, 2048], BF16)
    res = sb("res", [128, 4, D], F32)
    xT = sb("xT", [128, 8, STK], BF16)
    bigb = sb("bigb", [128, 16384], BF16)
    acc = sb("acc", [128, 4096], F32)
    rh = sb("rh", [128, 2, 512], F32)
    PT = sb("PT", [128, 4, 512], BF16)
    gb = sb("gb", [128, 2, D], F32)
    omix = sb("omix", [128, D], F32)
    omixg = sb("omixg", [128, 2, 512], F32)
    cf = sb("cf", [128, NCF], F32)
    identb = sb("identb", [128, 128], BF16)
    ident4 = sb("ident4", [128, 512], BF16)
    qgbd = sb("qgbd", [128, 2, 4, 256], BF16)
    kgT = sb("kgT", [128, 2, STK], BF16)
    ktok = sb("ktok", [128, 4, 256], BF16)
    vg = sb("vg", [128, 4, 512], BF16)
    gsil = sb("gsil", [128, 4, 512], BF16)
    ATt = sb("ATt", [128, 512], BF16)
    Sst = sb("Sst", [128, 2, 128], F32)
    Sbt = sb("Sbt", [128, 2, 128], BF16)
    wabs = sb("wabs", [128, 4, 8], F32)
    wsgn = sb("wsgn", [128, 4, 8], F32)
    small = sb("small", [128, 256], F32)
    tauc = sb("tauc", [128, 1], F32)
    seg8 = sb("seg8", [128, 32, 8], F32)

    aT = bigb[:, 0:NFC * 512].rearrange("p (f t) -> p f t", f=NFC)
    mbs = [bigb[:, 0:4096], bigb[:, 12288:16384]]
    qz = bigb[:, 4096:8192].rearrange("p (h t) -> p h t", h=8)
    qbd = bigb[:, 8192:12288].rearrange("p (a b c) -> p a b c", a=4, b=4)
    E1 = acc[:, 0:1024].rearrange("p (a t) -> p a t", a=2)
    E2 = acc[:, 1024:2048].rearrange("p (a t) -> p a t", a=2)
    E2tok = acc[:, 2048:3072].rearrange("p (a t) -> p a t", a=4)
    gaT = acc[:, 3072:3584]
    spt = acc[:, 3584:3840]
    gbv = gb[:, 0, :].bitcast(BF16)
    Dg = gbv[:, 0:1024].rearrange("p (h q) -> p h q", h=8)
    rhb = gb[:, 1, :].bitcast(BF16).rearrange("p (a t) -> p a t", a=4)
    identf = cf[:, CF_ID:CF_ID + 128]
    trin = cf[:, CF_TRIN:CF_TRIN + 128]
    trip = cf[:, CF_TRIP:CF_TRIP + 128]
    cmask = cf[:, CF_CM:CF_CM + 128]
    wa2p = cf[:, CF_WA2:CF_WA2 + 256]
    ba_b = cf[:, CF_BA:CF_BA + 256]
    gn = cf[:, CF_GN:CF_GN + 128]
    pow2 = cf[:, CF_P2:CF_P2 + KIT + 1]

    def sm(a, b):
        return small[:, a:b]

    ps = [es.enter_context(nc.psum_tensor("ps%d" % i, [128, 512], F32)) for i in range(8)]
    PB = [Buf("ps%d" % i, excl=True) for i in range(8)]

    B = {}

    def bf(name):
        if name not in B:
            B[name] = Buf(name)
        return B[name]

    cfB = bf("cf")
    resB = [bf("res%d" % i) for i in range(4)]
    xTB = [bf("xT%d" % i) for i in range(8)]
    ringB = [bf("ring%d" % i) for i in range(RING)]
    bigB = bf("bigb")
    aTB = [bf("aT%d" % i) for i in range(NFC)]
    accB = bf("acc")
    rhB = [bf("rh0"), bf("rh1")]
    PTB = [bf("PT%d" % i) for i in range(4)]
    gbB = bf("gb")
    gbP = bf("gbP")
    rhbB = [bf("rhb%d" % i) for i in range(4)]
    omixB = bf("omix")
    omixgB = [bf("omixg0"), bf("omixg1")]
    KcB = [bf("Kc%d" % i) for i in range(NST)]
    VcB = [bf("Vc%d" % i) for i in range(NST)]
    KixB = [bf("Kix%d" % i) for i in range(NST)]

    bank_rr = [0]

    def next_bank(pool=(0, 1, 2, 3, 4, 5, 6, 7)):
        bank_rr[0] += 1
        return pool[bank_rr[0] % len(pool)]

    def mm(out, lhsT, rhs, start, stop, reads, writes, skip=False):
        S.add("pe", lambda e: e.matmul(out, lhsT, rhs, start=start, stop=stop, skip_group_check=skip), reads, writes)

    def tr(out, in_, reads, writes):
        S.add("pe", lambda e: e.transpose(out, in_, identf), list(reads) + [cfB], writes)

    def act(out, in_, func, reads, writes, scale=1.0, bias=0.0):
        S.add("act", lambda e: e.activation(out, in_, func, bias=bias, scale=scale), reads, writes)

    def cp(eng, out, in_, reads, writes):
        if eng == "act":
            S.add("act", lambda e: e.activation(out, in_, AF.Copy), reads, writes)
        else:
            S.add(eng, lambda e: e.tensor_copy(out, in_), reads, writes)

    def tt(eng, out, in0, in1, op, reads, writes):
        S.add(eng, lambda e: e.tensor_tensor(out, in0, in1, op), reads, writes)

    def ts(eng, out, in0, s1, s2, op0, op1, reads, writes, accum=None):
        if op1 is None:
            S.add(eng, lambda e: e.tensor_scalar(out, in0, s1, None, op0), reads, writes)
        else:
            S.add(eng, lambda e: e.tensor_scalar(out, in0, s1, s2, op0, op1, accum_out=accum), reads, writes)

    def stt(out, in0, scalar, in1, op0, op1, reads, writes):
        S.add("dve", lambda e: e.scalar_tensor_tensor(out, in0, scalar, in1, op0, op1), reads, writes)

    evac_rr = [0]

    def evac_eng():
        evac_rr[0] += 1
        return "act" if evac_rr[0] % 2 else "dve"

    wstate = {"issued": 0, "cur": 0}
    total_chunks = NST * NCH

    def w_issue():
        j = wstate["issued"]
        if j >= total_chunks:
            return
        slot = j % RING
        src = ws_d[j % NCH]
        S.add("pool", lambda e: e.dma_start(out=ring[:, slot, :], in_=src, max_dma_last_dim=4096),
              reads=[], writes=[ringB[slot]], dma=ringB[slot])
        wstate["issued"] = j + 1

    def w_next():
        j = wstate["cur"]
        while wstate["issued"] < min(j + RING - 1, total_chunks) or wstate["issued"] <= j:
            w_issue()
        wstate["cur"] = j + 1
        slot = j % RING
        return ring[:, slot, :], ringB[slot]

    S.add("sp", lambda e: e.dma_start(out=cf[:], in_=cf_d), reads=[], writes=[cfB], dma=cfB)
    identB_ = bf("identb")
    cp("dve", identb[:], identf, [cfB], [identB_])
    for i in range(4):
        cp("dve", ident4[:, i * 128:(i + 1) * 128], identf, [cfB], [identB_])
    S.add("dve", lambda e: e.memset(bigb[:, 4096:12288], 0.0), [], [bigB])
    S.add("dve", lambda e: e.memset(Vc[:], 1.0), [], VcB)
    S.add("dve", lambda e: e.memset(Sst[:], 0.0), [], [bf("S")])
    S.add("dve", lambda e: e.memset(Sbt[:], 0.0), [], [bf("Sb")])
    S.add("dve", lambda e: e.memset(tauc[:], -1e29), [], [bf("tauc")])
    S.add("dve", lambda e: e.memset(qgbd[:], 0.0), [], [bf("qgbd")])

    def tile_to_xT(src, src_bufs, t4):
        tsl = slice(t4 * 128, (t4 + 1) * 128)
        for half in range(2):
            bank = next_bank((0, 1, 2, 3))
            for i in range(4):
                c = half * 4 + i
                tr(ps[bank][:, i * 128:(i + 1) * 128], src[:, c * 128:(c + 1) * 128], src_bufs, [PB[bank]])
            cp("act", xT[:, half * 4:half * 4 + 4, tsl], ps[bank][:].rearrange("p (a t) -> p a t", a=4),
               [PB[bank]], [xTB[half * 4 + i] for i in range(4)])

    def to_xT(_unused):
        for t4 in range(4):
            tile_to_xT(res[:, t4, :], [resB[t4]], t4)

    stg = [omix[:, :], rh[:, :, :].rearrange("p a t -> p (a t)")]
    stgB = [[omixB], [rhB[0], rhB[1]]]

    def prefetch_x_tile(s_next, t4):
        k = t4 % 2
        src = x_d[s_next * STK + t4 * 128:s_next * STK + (t4 + 1) * 128, :]
        S.add("sp", lambda e: e.dma_start(out=stg[k], in_=src), reads=[], writes=stgB[k], dma=bf("stgd%d" % k))
        tile_to_xT(stg[k], stgB[k], t4)

    ln_hoisted = set()

    def ln_stage(i, t4):
        st6 = sm(t4 * 16, t4 * 16 + 12)
        mv = sm(t4 * 16 + 12, t4 * 16 + 14)
        lnv = sm(t4 * 16 + 14, t4 * 16 + 15)
        rstd = sm(t4 * 16 + 15, t4 * 16 + 16)
        nmr = sm(160 + t4, 161 + t4)
        sB = bf("lnstat%d" % t4)
        r = res[:, t4, :]
        if i == -1:
            S.add("dve", lambda e: e.bn_stats(st6[:, 0:6], res[:, t4, 0:512]), [resB[t4]], [sB])
            ln_hoisted.add(t4)
        elif i == 0:
            if t4 in ln_hoisted:
                ln_hoisted.discard(t4)
            else:
                S.add("dve", lambda e: e.bn_stats(st6[:, 0:6], res[:, t4, 0:512]), [resB[t4]], [sB])
            S.add("dve", lambda e: e.bn_stats(st6[:, 6:12], res[:, t4, 512:1024]), [resB[t4]], [sB])
            S.add("dve", lambda e: e.bn_aggr(mv, st6), [sB], [sB])
            ts("dve", lnv, mv[:, 1:2], EPS_LN, None, ALU.add, None, [sB], [sB])
        elif i == 1:
            act(lnv, lnv, AF.Ln, [sB], [sB])
            act(rstd, lnv, AF.Exp, [sB], [sB], scale=-0.5)
        elif i == 2:
            stt(nmr, mv[:, 0:1], -1.0, rstd, ALU.mult, ALU.mult, [sB], [sB])
        elif i == 3:
            act(r, r, AF.Identity, [sB, resB[t4]], [resB[t4]], scale=rstd, bias=nmr)
        else:
            tt("dve", r, r, gb[:, 0, :], ALU.mult, [resB[t4], gbB, gbP], [resB[t4]])
            tt("dve", r, r, gb[:, 1, :], ALU.add, [resB[t4], gbB, gbP], [resB[t4]])

    def layer_norm_all(k, first=0):
        for step in range(first, 5 + 3):
            for t4 in range(4):
                i = step - t4
                if first <= i <= 4:
                    ln_stage(i, t4)

    def load_gb(k):
        for j in range(2):
            src = lnp_d[2 * k + j]
            S.add("sp", (lambda e, j=j, src=src: e.dma_start(out=gb[:, j, :], in_=src)), reads=[], writes=[gbB, gbP], dma=gbB)

    def ffn(k, down_hook=None, gu_hook=None, do_ln=True):
        for fc in range(NFC):
            w, wB = w_next()
            wv = w.rearrange("p (c f) -> p c f", c=8)
            bg = (0, 2)[fc % 2]
            bu = (1, 3)[fc % 2]
            for c in range(8):
                mm(ps[bg][:], wv[:, c, 0:128], xT[:, c, :], c == 0, c == 7, [wB, xTB[c]], [PB[bg]])
            for c in range(8):
                mm(ps[bu][:], wv[:, c, 128:256], xT[:, c, :], c == 0, c == 7, [wB, xTB[c]], [PB[bu]])
            sl = fc % 2
            act(rh[:, sl, :], ps[bg][:], AF.Silu, [PB[bg]], [rhB[sl]])
            tt("dve", aT[:, fc, :], rh[:, sl, :], ps[bu][:], ALU.mult, [rhB[sl], PB[bu], bigB], [aTB[fc]])
            if gu_hook is not None:
                gu_hook(fc)
        for dh in range(2):
            for g in range(6):
                w, wB = w_next()
                wv = w.rearrange("p (i f) -> p i f", i=4)
                for i in range(4):
                    fc = 4 * g + i
                    if fc >= NFC:
                        break
                    for t4 in range(4):
                        mm(ps[4 + t4][:], aT[:, fc, t4 * 128:(t4 + 1) * 128], wv[:, i, :], fc == 0, fc == NFC - 1,
                           [wB, aTB[fc]], [PB[4 + t4]])
                if down_hook is not None:
                    down_hook(dh, g)
            for t4 in range(4):
                r = res[:, t4, dh * 512:(dh + 1) * 512]
                stt(r, ps[4 + t4][:], C_FFN, r, ALU.mult, ALU.add, [PB[4 + t4], resB[t4]], [resB[t4]])
                if dh == 1 and do_ln:
                    ln_stage(0, t4)
            if dh == 0:
                for t4 in range(4):
                    ln_stage(-1, t4)
        if do_ln:
            layer_norm_all(k, first=1)

    def barrier_bigb(reads_extra=()):
        d = sm(200, 201)
        S.add("dve", lambda e: e.memset(d, 0.0), [], list(aTB) + [bigB, gbP, bf("dummy")])

    def zero_qpads():
        S.add("dve", lambda e: e.memset(bigb[:, 4096:12288], 0.0), [bigB], [bf("qz"), bf("qbd")])

    def projection(s, after_iw=None):
        tok0 = s * STK
        qgB, kgB, qbdB, qzB = bf("qgbd"), bf("kgT"), bf("qbd"), bf("qz")
        def fm_chunk(names):
            w, wB = w_next()
            wv = w.rearrange("p (c f) -> p c f", c=8)
            for half in range(2):
                name = names[half]
                bank = next_bank()
                for c in range(8):
                    mm(ps[bank][:], wv[:, c, half * 128:(half + 1) * 128], xT[:, c, :], c == 0, c == 7,
                       [wB, xTB[c]], [PB[bank]])
                pb = PB[bank]
                pv = ps[bank]
                if name == "ga":
                    cp("act", gaT, pv[:], [pb], [accB])
                    for t4 in range(4):
                        b2 = next_bank()
                        mm(ps[b2][:, 0:256], gaT[:, t4 * 128:(t4 + 1) * 128], wa2p, True, True, [accB, cfB], [PB[b2]])
                        tt("dve", spt, ps[b2][:, 0:256], ba_b, ALU.add, [PB[b2], cfB], [accB])
                        act(spt, spt, AF.Exp, [accB], [accB], scale=-1.0)
                        ts("dve", spt, spt, 1.0, None, ALU.add, None, [accB], [accB])
                        act(spt, spt, AF.Ln, [accB], [accB])
                        b3 = next_bank()
                        for p in range(2):
                            mm(ps[b3][:, p * 128:(p + 1) * 128], spt[:, p * 128:(p + 1) * 128], trin, p == 0, p == 1,
                               [accB, cfB], [PB[b3]])
                        for p in range(2):
                            act(sm(120 + p * 4 + t4, 121 + p * 4 + t4), ps[b3][:, p * 128 + 127:p * 128 + 128], AF.Exp,
                                [PB[b3]], [bf("ebl")])
                            act(E1[:, p, t4 * 128:(t4 + 1) * 128], ps[b3][:, p * 128:(p + 1) * 128], AF.Exp,
                                [PB[b3]], [accB])
                            act(E2[:, p, t4 * 128:(t4 + 1) * 128], ps[b3][:, p * 128:(p + 1) * 128], AF.Exp,
                                [PB[b3]], [accB], scale=-1.0)
                        b4 = next_bank()
                        mm(ps[b4][:, 0:256], trip, spt, True, True, [accB, cfB], [PB[b4]])
                        act(E2tok[:, t4, :], ps[b4][:, 0:256], AF.Exp, [PB[b4]], [accB])
                elif name == "ik":
                    cp(evac_eng(), Kix[:, tok0:tok0 + STK], pv[:], [pb], [KixB[s]])
                elif name.startswith("gq"):
                    p = int(name[2])
                    for hh in range(2):
                        rows = slice(hh * 64, (hh + 1) * 64)
                        stt(qgbd[rows, p, :, hh * 128:(hh + 1) * 128],
                            pv[rows, :].rearrange("p (a t) -> p a t", a=4), 0.125,
                            E1[rows, p, :].rearrange("p (a t) -> p a t", a=4), ALU.mult, ALU.mult,
                            [pb, accB], [qgB])
                elif name.startswith("gk"):
                    p = int(name[2])
                    tt("dve", kgT[:, p, :], pv[:], E2[:, p, :], ALU.mult, [pb, accB], [kgB])
                elif name.startswith("dq"):
                    p = int(name[2])
                    for hh in range(2):
                        rows = slice(hh * 64, (hh + 1) * 64)
                        cp("act", qbd[rows, p, :, hh * 128:(hh + 1) * 128],
                           pv[rows, :].rearrange("p (a t) -> p a t", a=4), [pb, bigB], [qbdB])
                elif name.startswith("dk"):
                    p = int(name[2])
                    cp("act", Kc[:, p, tok0:tok0 + STK], pv[:], [pb], [KcB[s]])
                elif name.startswith("iq"):
                    p = int(name[2])
                    for hh in range(2):
                        rows = slice(hh * 64, (hh + 1) * 64)
                        cp(evac_eng(), qz[rows, 2 * p + hh, :], pv[rows, :], [pb, bigB], [qzB])
        ktB, vgB, gsB, wB_ = bf("ktok"), bf("vg"), bf("gsil"), bf("wabs")

        def tm_chunk(name):
            w, wB = w_next()
            wv = w.rearrange("p (c f) -> p c f", c=8)
            ncol = 8 if name == "iw" else 256
            for t4 in range(4):
                bank = next_bank()
                for c in range(8):
                    mm(ps[bank][:, 0:ncol], xT[:, c, t4 * 128:(t4 + 1) * 128], wv[:, c, 0:ncol], c == 0, c == 7,
                       [wB, xTB[c]], [PB[bank]])
                pb = PB[bank]
                pv = ps[bank][:, 0:ncol]
                if name == "gk":
                    tt("dve", ktok[:, t4, :], pv, E2tok[:, t4, :], ALU.mult, [pb, accB], [ktB])
                elif name.startswith("gv"):
                    i = int(name[2])
                    cp("act", vg[:, t4, i * 256:(i + 1) * 256], pv, [pb], [vgB])
                elif name.startswith("gg"):
                    i = int(name[2])
                    sl = t4 % 2
                    act(rh[:, sl, 0:256], pv, AF.Silu, [pb], [rhB[sl]])
                    tt("dve", gsil[:, t4, i * 256:(i + 1) * 256].rearrange("p (h v) -> p h v", h=2),
                       rh[:, sl, 0:256].rearrange("p (h v) -> p h v", h=2),
                       gn.unsqueeze(1).to_broadcast([128, 2, 128]), ALU.mult, [rhB[sl], cfB], [gsB])
                elif name.startswith("dv"):
                    i = int(name[2])
                    dst = Vc[:, s * 4 + t4, :].rearrange("p (h e) -> p h e", e=65)[:, 4 * i:4 * i + 4, 0:64]
                    cp("act", dst, pv.rearrange("p (h e) -> p h e", e=64), [pb], [VcB[s]])
                elif name == "iw":
                    ts("dve", wsgn[:, t4, :], pv, 0.0, 2.0, ALU.is_ge, ALU.mult, [pb], [wB_])
                    ts("dve", wsgn[:, t4, :], wsgn[:, t4, :], -1.0, None, ALU.add, None, [wB_], [wB_])
                    stt(wabs[:, t4, :], pv, C_IDX, wsgn[:, t4, :], ALU.mult, ALU.mult, [pb, wB_], [wB_])

        for names in (("ga", "ik"), ("gq0", "gq1"), ("gk0", "gk1"), ("iq0", "iq1"), ("iq2", "iq3")):
            fm_chunk(names)
        tm_chunk("gk")
        tm_chunk("gg0")
        tm_chunk("gg1")
        tm_chunk("iw")
        if after_iw is not None:
            after_iw()
        for names in (("dq0", "dq1"), ("dq2", "dq3"), ("dk0", "dk1"), ("dk2", "dk3")):
            fm_chunk(names)
        for name in ("gv0", "gv1", "dv0", "dv1"):
            tm_chunk(name)

    def gla(s, t4):
        qgB, kgB, ktB, vgB, gsB = bf("qgbd"), bf("kgT"), bf("ktok"), bf("vg"), bf("gsil")
        SB_, SbB, ATB = bf("S"), bf("Sb"), bf("AT")
        tsl = slice(t4 * 128, (t4 + 1) * 128)
        bA = next_bank()
        for p in range(2):
            mm(ps[bA][:, p * 256:(p + 1) * 256], kgT[:, p, tsl], qgbd[:, p, t4, :], p == 0, p == 1,
               [kgB, qgB], [PB[bA]])
        tt("dve", ATt[:].rearrange("p (h t) -> p h t", h=4), ps[bA][:].rearrange("p (h t) -> p h t", h=4),
           cmask.unsqueeze(1).to_broadcast([128, 4, 128]), ALU.mult, [PB[bA], cfB], [ATB])
        bo = next_bank()
        for h in range(4):
            p = h // 2
            mm(ps[bo][:, h * 128:(h + 1) * 128], ATt[:, h * 128:(h + 1) * 128], vg[:, t4, h * 128:(h + 1) * 128],
               h == 0, False, [ATB, vgB], [PB[bo]])
            mm(ps[bo][:, h * 128:(h + 1) * 128], qgbd[:, p, t4, (h % 2) * 128:(h % 2 + 1) * 128], Sbt[:, p, :],
               False, h == 3, [qgB, SbB], [PB[bo]])
        bS = next_bank()
        for p in range(2):
            mm(ps[bS][:, p * 256:(p + 1) * 256], ktok[:, t4, p * 128:(p + 1) * 128], vg[:, t4, p * 256:(p + 1) * 256],
               p == 0, p == 1, [ktB, vgB], [PB[bS]])
        for p in range(2):
            for hh in range(2):
                rows = slice(hh * 64, (hh + 1) * 64)
                tt("dve", Sst[rows, p, :], Sst[rows, p, :], ps[bS][rows, p * 256 + hh * 128:p * 256 + (hh + 1) * 128],
                   ALU.add, [SB_, PB[bS]], [SB_])
                ts("dve", Sst[rows, p, :], Sst[rows, p, :], small[rows, 120 + p * 4 + t4:121 + p * 4 + t4], None,
                   ALU.mult, None, [SB_, bf("ebl")], [SB_])
            cp("act", Sbt[:, p, :], Sst[:, p, :], [SB_], [SbB])
        ot = rh[:, 0, :]
        sq = rh[:, 1, :]
        ssq = sm(64, 68)
        rs = sm(68, 72)
        gB = bf("glastat")
        cp("act", ot, ps[bo][:], [PB[bo]], [rhB[0]])
        tt("dve", sq, ot, ot, ALU.mult, [rhB[0]], [rhB[1]])
        S.add("dve", lambda e: e.tensor_reduce(ssq, sq.rearrange("p (h v) -> p h v", h=4), AX.X, ALU.add),
              [rhB[1]], [gB])
        ts("dve", ssq, ssq, 1.0 / 128.0, RMS_EPS, ALU.mult, ALU.add, [gB], [gB])
        act(ssq, ssq, AF.Ln, [gB], [gB])
        act(rs, ssq, AF.Exp, [gB], [gB], scale=-0.5)
        tt("dve", ot.rearrange("p (h v) -> p h v", h=4), ot.rearrange("p (h v) -> p h v", h=4),
           rs.unsqueeze(2).to_broadcast([128, 4, 128]), ALU.mult, [rhB[0], gB], [rhB[0]])
        tt("dve", omixg[:, t4 % 2, :], ot, gsil[:, t4, :], ALU.mult, [rhB[0], gsB], [omixgB[t4 % 2]])

    def dsa_select(s, t4):
        qb = s * 4 + t4
        N = (qb + 1) * 128
        nkt = (N + 511) // 512
        tsl = slice(t4 * 128, (t4 + 1) * 128)
        mb = mbs[qb % 2]
        qzB, wB_, mbB, tauB = bf("qz"), bf("wabs"), bf("mb%d" % (qb % 2)), bf("tau")
        kixr = [KixB[i] for i in range(s + 1)]
        DB = bf("Dg")
        for h in range(8):
            ts("dve", Dg[:, h, :], identb[:], wsgn[:, t4, h:h + 1], None, ALU.mult, None,
               [bf("identb"), wB_, gbP], [DB])
        LAG = 3
        steps = [(kt, h) for kt in range(nkt) for h in range(8)]
        info = {}

        def score(i):
            kt, h = steps[i]
            ncol = min(512, N - kt * 512)
            bank = next_bank((2, 3, 4, 5, 6, 7))
            sl = i % 4
            mm(ps[bank][:, 0:ncol], qz[:, h, tsl], Kix[:, kt * 512:kt * 512 + ncol], True, True, [qzB] + kixr,
               [PB[bank]])
            if i % 2 == 0:
                act(rhb[:, sl, 0:ncol], ps[bank][:, 0:ncol], AF.Relu, [PB[bank], wB_, gbP], [rhbB[sl]],
                    scale=wabs[:, t4, h:h + 1])
            else:
                ts("dve", rhb[:, sl, 0:ncol], ps[bank][:, 0:ncol], wabs[:, t4, h:h + 1], 0.0, ALU.mult, ALU.max,
                   [PB[bank], wB_, gbP], [rhbB[sl]])

        def hsum(i):
            kt, h = steps[i]
            ncol = min(512, N - kt * 512)
            bsum = (0, 1)[kt % 2]
            sl = i % 4
            mm(ps[bsum][:, 0:ncol], Dg[:, h, :], rhb[:, sl, 0:ncol], h == 0, h == 7, [DB, rhbB[sl], gbP], [PB[bsum]])
            if h == 7:
                cp("act", acc[:, kt * 512:kt * 512 + ncol], ps[bsum][:, 0:ncol], [PB[bsum]], [accB])

        for i in range(len(steps) + LAG):
            if i < len(steps):
                score(i)
            if i >= LAG:
                hsum(i - LAG)
        a = acc[:, 0:N]
        if qb >= 2:
            mn = sm(80, 81)
            mx8 = sm(88, 96)
            cc = sm(81, 82)
            h0 = sm(82, 83)
            cnt = sm(83, 84)
            pm = sm(84, 85)
            hk = sm(96, 96 + KIT + 1)
            tau = sm(85, 86)
            S.add("dve", lambda e: e.memset(acc[0:64, N - 64:N], -1e30), [], [accB])
            segv = a.rearrange("p (j g) -> p g j", g=32)
            for g in range(32):
                S.add("dve", lambda e, g=g: e.max(seg8[:, g, :], segv[:, g, :]), [accB], [bf("seg%d" % g)])
            segB = [bf("seg%d" % g) for g in range(32)]
            S.add("dve", lambda e: e.tensor_reduce(mn, seg8[:, :, 7], AX.X, ALU.min), segB, [tauB])
            S.add("dve", lambda e: e.tensor_reduce(mx8[:, 0:1], seg8[:, :, 7], AX.X, ALU.max), segB, [tauB])
            tt("dve", cc, mx8[:, 0:1], mn, ALU.add, [tauB], [tauB])
            ts("dve", cc, cc, 0.5, None, ALU.mult, None, [tauB], [tauB])
            tt("dve", h0, mx8[:, 0:1], mn, ALU.subtract, [tauB], [tauB])
            ts("dve", h0, h0, 0.5, None, ALU.mult, None, [tauB], [tauB])
            ts("dve", hk, pow2, h0, None, ALU.mult, None, [tauB, cfB], [tauB])
            for k in range(KIT):
                ts("dve", mb[:, 0:N], a, cc, 0.0, ALU.is_ge, ALU.add, [accB, tauB, bigB], [mbB, tauB], accum=cnt)
                ts("dve", pm, cnt, TOPK - 0.5, 0.5, ALU.is_ge, ALU.subtract, [tauB], [tauB])
                stt(cc, pm, hk[:, k:k + 1], cc, ALU.mult, ALU.add, [tauB], [tauB])
            tt("dve", tau, cc, hk[:, KIT:KIT + 1], ALU.subtract, [tauB], [tauB])
        else:
            S.add("dve", lambda e: e.memset(acc[0:64, N - 64:N], -1e30), [], [accB])
            tau = tauc[:, 0:1]
        ts("dve", mb[:, 0:N], a, tau, NEG, ALU.is_lt, ALU.mult, [accB, tauB, bf("tauc"), bigB], [mbB])

    def dsa_attend(s, t4):
        qb = s * 4 + t4
        mb = mbs[qb % 2]
        qbdB, mbB = bf("qbd"), bf("mb%d" % (qb % 2))
        kcr = [KcB[i] for i in range(s + 1)]
        vcr = [VcB[i] for i in range(s + 1)]
        pool_l = (2, 3, 4, 5, 6, 7)
        units = [(st, bk) for st in range(qb + 1) for bk in range(2)]
        LAG = 2
        ubank = {}

        def logits(u):
            st, bk = units[u]
            ssl = slice(st * 128, (st + 1) * 128)
            bank = next_bank(pool_l)
            ubank[u] = bank
            mm(ps[bank][:], mb[:, ssl], ident4[:], True, False, [mbB, bf("identb")], [PB[bank]])
            for pp in range(2):
                pair = 2 * bk + pp
                mm(ps[bank][:, pp * 256:(pp + 1) * 256], Kc[:, pair, ssl], qbd[:, pair, t4, :], False, pp == 1,
                   kcr + [qbdB], [PB[bank]])
            slot = u % 4
            act(PT[:, slot, :], ps[bank][:], AF.Exp, [PB[bank]], [PTB[slot]], scale=0.125)

        def pv(u):
            st, bk = units[u]
            slot = u % 4
            for hh in range(4):
                h = 4 * bk + hh
                mm(ps[bk][:, hh * 65:(hh + 1) * 65], PT[:, slot, hh * 128:(hh + 1) * 128],
                   Vc[:, st, h * 65:(h + 1) * 65], st == 0 and hh == 0, st == qb and hh == 3,
                   [PTB[slot]] + vcr, [PB[bk]])

        for u in range(len(units) + LAG):
            if u < len(units):
                logits(u)
            if u >= LAG:
                pv(u - LAG)
        for bk in range(2):
            rden = sm(72 + 4 * bk, 76 + 4 * bk)
            dB = bf("rden%d" % bk)
            pv = ps[bk][:, 0:260].rearrange("p (h e) -> p h e", e=65)
            S.add("dve", lambda e, rden=rden, pv=pv: e.reciprocal(rden.unsqueeze(2), pv[:, :, 64:65]), [PB[bk]], [dB])
            tt("dve", omix[:, 512 + bk * 256:512 + (bk + 1) * 256].rearrange("p (h e) -> p h e", e=64),
               pv[:, :, 0:64], rden.unsqueeze(2).to_broadcast([128, 4, 64]), ALU.mult, [PB[bk], dB], [omixB])

    def omix_to_xT(t4):
        tsl = slice(t4 * 128, (t4 + 1) * 128)
        for half in range(2):
            bank = next_bank((2, 3, 4, 5, 6, 7))
            for i in range(4):
                if half == 0:
                    tr(ps[bank][:, i * 128:(i + 1) * 128], omixg[:, t4 % 2, i * 128:(i + 1) * 128],
                       [omixgB[t4 % 2]], [PB[bank]])
                else:
                    tr(ps[bank][:, i * 128:(i + 1) * 128], omix[:, 512 + i * 128:512 + (i + 1) * 128],
                       [omixB], [PB[bank]])
            cp("act", xT[:, half * 4:half * 4 + 4, tsl], ps[bank][:].rearrange("p (a t) -> p a t", a=4),
               [PB[bank]], [xTB[half * 4 + i] for i in range(4)])

    def wout():
        for j in range(4):
            w, wB = w_next()
            wv = w.rearrange("p (c f) -> p c f", c=8)
            for t4 in range(4):
                bank = next_bank()
                for c in range(8):
                    mm(ps[bank][:, 0:256], xT[:, c, t4 * 128:(t4 + 1) * 128], wv[:, c, :], c == 0, c == 7,
                       [wB, xTB[c]], [PB[bank]])
                r = res[:, t4, j * 256:(j + 1) * 256]
                stt(r, ps[bank][:, 0:256], C_MIX, r, ALU.mult, ALU.add, [PB[bank], resB[t4]], [resB[t4]])
                if j == 3:
                    ln_stage(0, t4)
            if j == 1:
                for t4 in range(4):
                    ln_stage(-1, t4)

    def dump(dst, s):
        S.add("sp", lambda e: e.dma_start(out=dst[s * STK:(s + 1) * STK, :].rearrange("(a p) d -> p a d", p=128),
                                          in_=res[:]), reads=resB, writes=[], dma=bf("dump"))

    deferred = [None]
    for s in range(NST):
        xin = x_d[s * STK:(s + 1) * STK, :].rearrange("(a p) d -> p a d", p=128)
        if s == 0:
            S.add("sp", lambda e, xin=xin: e.dma_start(out=res[:], in_=xin), reads=[], writes=resB, dma=bf("resld"))
            load_gb(0)
            to_xT(None)
        if stop >= 1:
            def gu_hook(fc):
                if fc >= 1 and deferred[0]:
                    deferred[0].pop(0)()
            ffn(0, gu_hook=gu_hook)
        if dbg:
            dump(dbg1_d, s)
        if stop >= 2:
            barrier_bigb()
            to_xT(None)
            zero_qpads()
            projection(s, after_iw=(lambda s=s: dsa_select(s, 0)))
            gla(s, 0)
        for t4 in range(4):
            if stop >= 3:
                if t4 < 3:
                    gla(s, t4 + 1)
                    dsa_select(s, t4 + 1)
                dsa_attend(s, t4)
                omix_to_xT(t4)
            if dbg and stop >= 3:
                S.add("sp", lambda e, s=s, t4=t4: e.dma_start(
                    out=dbg2_d[s * STK + t4 * 128:s * STK + (t4 + 1) * 128, :], in_=omix[:]),
                    reads=[omixB], writes=[], dma=bf("dump2"))
        if stop >= 6:
            load_gb(1)
            wout()
            layer_norm_all(1, first=1)
        if dbg:
            dump(dbg3_d, s)
        if stop >= 7:
            load_gb(2)
            to_xT(None)
            S.add("dve", lambda e: e.memset(sm(201, 202), 0.0), [], [bf("mb0"), bf("mb1"), bf("qz"), bf("qbd"), bigB, bf("dummy")])
            def hook(dh, g, s=s):
                if s + 1 < NST and dh == 0 and 1 <= g <= 4:
                    prefetch_x_tile(s + 1, g - 1)
            last = (s + 1 >= NST)
            ffn(2, down_hook=hook, do_ln=last)

        def store_out(s=s):
            S.add("sp", lambda e: e.dma_start(
                out=out_d[s * STK:(s + 1) * STK, :].rearrange("(a p) d -> p a d", p=128), in_=res[:]),
                reads=resB, writes=[], dma=bf("outst"))

        if s + 1 >= NST or stop < 7:
            store_out()
        else:
            def make_tail(s=s, store_out=store_out):
                steps = []
                for i in range(5):
                    steps.append(lambda i=i: [ln_stage(i, t4) for t4 in range(4)])

                def fin():
                    store_out()
                    xn = x_d[(s + 1) * STK:(s + 2) * STK, :].rearrange("(a p) d -> p a d", p=128)
                    S.add("sp", lambda e: e.dma_start(out=res[:], in_=xn), reads=[], writes=resB, dma=bf("resld"))
                    load_gb(0)
                steps.append(fin)
                return steps
            deferred[0] = make_tail()
    fin = []
    for name in (["outst"] + (["dump", "dump2"] if dbg else [])):
        b = bf(name)
        fin.append(b)
    S.add("sp", lambda e: e.nop(), reads=[], writes=fin + resB + [omixB] + omixgB)
    S.emit(nc, es)
    es.close()
    return nc


def _chunk_cols(w, cols_list):
    out = np.zeros((128, 8, 256), np.float32)
    o = 0
    for cols in cols_list:
        sel = w[:, cols]
        n = sel.shape[1]
        out[:, :, o:o + n] = sel.reshape(8, 128, n).transpose(1, 0, 2)
        o += n
    return out.reshape(128, 2048)


def _build_ws(w_in, w_out, w_gu1, w_d1, w_gu2, w_d2):
    chunks = []

    def ffn_chunks(w_gu, w_d):
        for fc in range(NFC):
            chunks.append(_chunk_cols(w_gu, [np.arange(fc * 128, (fc + 1) * 128),
                                             np.arange(DFF + fc * 128, DFF + (fc + 1) * 128)]))
        for dh in range(2):
            for g in range(6):
                a = np.zeros((128, 4, 512), np.float32)
                for i in range(4):
                    fc = 4 * g + i
                    if fc < NFC:
                        a[:, i, :] = w_d[fc * 128:(fc + 1) * 128, dh * 512:(dh + 1) * 512]
                chunks.append(a.reshape(128, 2048))

    ffn_chunks(w_gu1, w_d1)
    r = np.arange
    ik = np.concatenate([r(3600, 3664), r(3600, 3664)])
    fmA = [r(1536, 1664), ik, r(0, 128), r(128, 256), r(256, 384), r(384, 512)]
    fmA += [r(3088 + p * 128, 3088 + (p + 1) * 128) for p in range(4)]
    for i in range(5):
        chunks.append(_chunk_cols(w_in, [fmA[2 * i], fmA[2 * i + 1]]))
    for cols in (r(256, 512), r(1024, 1280), r(1280, 1536), r(3664, 3672)):
        chunks.append(_chunk_cols(w_in, [cols]))
    fmB = [r(1552 + p * 128, 1552 + (p + 1) * 128) for p in range(4)]
    fmB += [r(2064 + p * 128, 2064 + (p + 1) * 128) for p in range(4)]
    for i in range(4):
        chunks.append(_chunk_cols(w_in, [fmB[2 * i], fmB[2 * i + 1]]))
    for cols in (r(512, 768), r(768, 1024), r(2576, 2832), r(2832, 3088)):
        chunks.append(_chunk_cols(w_in, [cols]))
    for j in range(4):
        chunks.append(_chunk_cols(w_out, [r(j * 256, (j + 1) * 256)]))
    ffn_chunks(w_gu2, w_d2)
    assert len(chunks) == NCH
    return np.ascontiguousarray(np.stack(chunks, 0))


def _build_cf(w_gla_a2, b_gla_a, gla_norm_g):
    cf = np.zeros((128, NCF), np.float32)
    cf[:, CF_ID:CF_ID + 128] = np.eye(128, dtype=np.float32)
    tp = np.arange(128)[:, None]
    t = np.arange(128)[None, :]
    le = (tp <= t).astype(np.float32)
    cf[:, CF_TRIN:CF_TRIN + 128] = le * np.float32(-1.0 / 16.0)
    cf[:, CF_TRIP:CF_TRIP + 128] = le * np.float32(1.0 / 16.0)
    cf[:, CF_CM:CF_CM + 128] = le
    cf[0:16, CF_WA2:CF_WA2 + 256] = w_gla_a2
    cf[:, CF_BA:CF_BA + 256] = b_gla_a[None, :]
    cf[:, CF_GN:CF_GN + 128] = gla_norm_g[None, :]
    cf[:, CF_P2:CF_P2 + KIT + 1] = (2.0 ** -np.arange(KIT + 1, dtype=np.float64)).astype(np.float32)[None, :]
    return cf


_NC_CACHE = {}


def _run(inputs, T, dbg=False, trace=False, stop=99):
    x = np.asarray(inputs["x"], np.float32)
    ws = _build_ws(np.asarray(inputs["w_in"][0], np.float32), np.asarray(inputs["w_out"][0], np.float32),
                   np.asarray(inputs["ffn1_w_gu"][0], np.float32), np.asarray(inputs["ffn1_w_down"][0], np.float32),
                   np.asarray(inputs["ffn2_w_gu"][0], np.float32), np.asarray(inputs["ffn2_w_down"][0], np.float32))
    cf = _build_cf(np.asarray(inputs["w_gla_a2"][0], np.float32), np.asarray(inputs["b_gla_a"][0], np.float32),
                   np.asarray(inputs["gla_norm_g"][0], np.float32))
    lnp = np.stack([np.asarray(inputs[k][0], np.float32) for k in
                    ("ln1_g", "ln1_b", "ln2_g", "ln2_b", "ln3_g", "ln3_b")], 0)
    lnp = np.ascontiguousarray(np.broadcast_to(lnp[:, None, :], (6, 128, D)))
    key = (T, dbg, stop)
    if key not in _NC_CACHE:
        _NC_CACHE[key] = build(T, dbg, stop)
    nc = _NC_CACHE[key]
    nb = x.shape[0]
    in_maps = [{"x": np.ascontiguousarray(x[b, :T]), "ws": ws, "cf": cf, "lnp": lnp} for b in range(nb)]
    res = run_bass_kernel_spmd(nc, in_maps, core_ids=list(range(nb)), trace=trace)
    return res


def kernel(**inputs):
    res = _run(inputs, SEQ)
    out = np.stack([np.asarray(r["out"], np.float32) for r in res.results], 0)
    return out
```
